# Optimizing a Trainium2 kernel written in Bass

```python
import jax, jax.numpy as jnp
from jax import lax
import numpy as np

D_MODEL = 2048
BATCH = 4
SEQ = 4096
DEPTH = 1

PLE_DIM = 256
CONV_WIDTH = D_MODEL // 2
CONV_K = 3
HG_DK = 128
HG_DV = 128
HG_HEADS = (D_MODEL // 2) // HG_DV
CHUNK = 32
D_FF = ((8 * D_MODEL // 3 + 127) // 128) * 128
LN_EPS = 1e-5
RMS_EPS = 1e-6
ALPHA = (2.0 * DEPTH) ** 0.25
BETA = (8.0 * DEPTH) ** -0.25
MIX_SIZES = (CONV_WIDTH,) * 3 + (HG_HEADS * HG_DK,) * 2 + (HG_HEADS * HG_DV,) * 2 + (D_MODEL,) * 2
MIX_COLS = sum(MIX_SIZES)

kernel_name = "hybrid_conv_hgrn2_macaron_deepnorm"


def _split_points():
    return [int(s) for s in np.cumsum(MIX_SIZES)[:-1]]


def layer_norm(x, g, b):
    xf = x.astype(jnp.float32)
    mu = xf.mean(-1, keepdims=True)
    var = jnp.square(xf - mu).mean(-1, keepdims=True)
    y = (xf - mu) * lax.rsqrt(var + LN_EPS) * g.astype(jnp.float32) + b.astype(jnp.float32)
    return y.astype(x.dtype)


def swiglu(x, w_in, w_out):
    a, u = jnp.split(x @ w_in, 2, axis=-1)
    return (jax.nn.silu(a) * u) @ w_out


def causal_dwconv(u, w):
    return lax.conv_general_dilated(
        u, w[:, None, :].astype(u.dtype), window_strides=(1,), padding=[(CONV_K - 1, 0)],
        dimension_numbers=("NWC", "WIO", "NWC"), feature_group_count=u.shape[-1])


def short_conv_mixer(b_gate, c_gate, h, w_conv):
    return b_gate * causal_dwconv(c_gate * h, w_conv)


def chunked_gla(q, k, v, logf):
    bsz, s, h, dk = q.shape
    dv = v.shape[-1]
    n = s // CHUNK

    def to_chunks(t):
        return t.reshape(bsz, n, CHUNK, h, t.shape[-1]).transpose(1, 0, 3, 2, 4)

    qc, kc, vc, gc = to_chunks(q), to_chunks(k), to_chunks(v), to_chunks(logf)
    causal = jnp.tril(jnp.ones((CHUNK, CHUNK), dtype=bool))[:, :, None]

    def step(state, inp):
        qb, kb, vb, gb = inp
        cum = jnp.cumsum(gb, axis=2)
        o_inter = jnp.einsum('bhck,bhkv->bhcv', qb * jnp.exp(cum), state)
        diff = cum[:, :, :, None, :] - cum[:, :, None, :, :]
        decay = jnp.exp(jnp.where(causal, diff, -jnp.inf))
        scores = jnp.einsum('bhtk,bhsk,bhtsk->bhts', qb, kb, decay)
        o_intra = jnp.einsum('bhts,bhsv->bhtv', scores, vb)
        last = cum[:, :, -1:, :]
        new_state = (jnp.exp(last[:, :, 0, :])[..., None] * state
                     + jnp.einsum('bhsk,bhsv->bhkv', kb * jnp.exp(last - cum), vb))
        return new_state, o_inter + o_intra

    s0 = jnp.zeros((bsz, h, dk, dv), jnp.float32)
    _, o = lax.scan(step, s0, (qc, kc, vc, gc))
    return o.transpose(1, 0, 3, 2, 4).reshape(bsz, s, h, dv)


def hgrn2_mixer(q_raw, f_raw, i_in, g_raw, lower_bound, norm_w):
    bsz, s, _ = q_raw.shape
    f32 = jnp.float32
    q = jax.nn.silu(q_raw.astype(f32)).reshape(bsz, s, HG_HEADS, HG_DK)
    lb = lower_bound.reshape(HG_HEADS, HG_DK)
    f = lb + (1.0 - lb) * jax.nn.sigmoid(f_raw.astype(f32)).reshape(bsz, s, HG_HEADS, HG_DK)
    k = 1.0 - f
    v = i_in.astype(f32).reshape(bsz, s, HG_HEADS, HG_DV)
    o = chunked_gla(q, k, v, jnp.log(f))
    o = o * lax.rsqrt(jnp.mean(jnp.square(o), -1, keepdims=True) + RMS_EPS) * norm_w.astype(f32)
    o = o * jax.nn.silu(g_raw.astype(f32).reshape(bsz, s, HG_HEADS, HG_DV))
    return o.reshape(bsz, s, HG_HEADS * HG_DV).astype(q_raw.dtype)


def setup_inputs(seed: int = 0) -> dict:
    key = jax.random.key(seed)
    ks = jax.random.split(key, 18)
    nrm = lambda k, shape, scale: jax.random.normal(k, shape, jnp.float32) * scale
    L = DEPTH
    return {
        "x": nrm(ks[0], (BATCH, SEQ, D_MODEL), 1.0),
        "p": nrm(ks[1], (L, BATCH, SEQ, PLE_DIM), 1.0),
        "ln_g": 1.0 + nrm(ks[2], (L, 4, D_MODEL), 0.02),
        "ln_b": nrm(ks[3], (L, 4, D_MODEL), 0.02),
        "ffn1_w_in": nrm(ks[4], (L, D_MODEL, 2 * D_FF), D_MODEL ** -0.5),
        "ffn1_w_out": nrm(ks[5], (L, D_FF, D_MODEL), BETA * D_FF ** -0.5),
        "mix_w_in": nrm(ks[6], (L, D_MODEL, MIX_COLS), D_MODEL ** -0.5),
        "conv_w": nrm(ks[7], (L, CONV_K, CONV_WIDTH), CONV_K ** -0.5),
        "hg_lower_bound": nrm(ks[8], (L + 1, HG_HEADS * HG_DK), 0.1),
        "hg_norm_w": 1.0 + nrm(ks[9], (L, HG_DV), 0.02),
        "branch_w_conv": nrm(ks[10], (L, CONV_WIDTH, D_MODEL), BETA * CONV_WIDTH ** -0.5),
        "branch_w_hgrn": nrm(ks[11], (L, HG_HEADS * HG_DV, D_MODEL), BETA * (HG_HEADS * HG_DV) ** -0.5),
        "mix_w_out": nrm(ks[12], (L, D_MODEL, D_MODEL), BETA * D_MODEL ** -0.5),
        "ffn2_w_in": nrm(ks[13], (L, D_MODEL, 2 * D_FF), D_MODEL ** -0.5),
        "ffn2_w_out": nrm(ks[14], (L, D_FF, D_MODEL), BETA * D_FF ** -0.5),
        "ple_w_gate": nrm(ks[15], (L, D_MODEL, D_MODEL), D_MODEL ** -0.5),
        "ple_w_proj": nrm(ks[16], (L, PLE_DIM, D_MODEL), BETA * PLE_DIM ** -0.5),
    }


def reference(x, p, ln_g, ln_b, ffn1_w_in, ffn1_w_out, mix_w_in, conv_w, hg_lower_bound,
              hg_norm_w, branch_w_conv, branch_w_hgrn, mix_w_out, ffn2_w_in, ffn2_w_out,
              ple_w_gate, ple_w_proj):
    lower_bounds = jnp.cumsum(jax.nn.softmax(hg_lower_bound.astype(jnp.float32), axis=0), axis=0)
    splits = _split_points()
    for i in range(DEPTH):
        x = layer_norm(ALPHA * x + 0.5 * swiglu(x, ffn1_w_in[i], ffn1_w_out[i]), ln_g[i, 0], ln_b[i, 0])
        z = x @ mix_w_in[i]
        b_gate, c_gate, h_conv, q_raw, f_raw, i_in, g_raw, gate_conv, gate_hgrn = jnp.split(z, splits, axis=-1)
        y_conv = short_conv_mixer(b_gate, c_gate, h_conv, conv_w[i])
        y_hgrn = hgrn2_mixer(q_raw, f_raw, i_in, g_raw, lower_bounds[i], hg_norm_w[i])
        merged = (jax.nn.sigmoid(gate_conv) * (y_conv @ branch_w_conv[i])
                  + jax.nn.sigmoid(gate_hgrn) * (y_hgrn @ branch_w_hgrn[i]))
        x = layer_norm(ALPHA * x + merged @ mix_w_out[i], ln_g[i, 1], ln_b[i, 1])
        x = layer_norm(ALPHA * x + 0.5 * swiglu(x, ffn2_w_in[i], ffn2_w_out[i]), ln_g[i, 2], ln_b[i, 2])
        ple = jax.nn.sigmoid(x @ ple_w_gate[i]) * (p[i] @ ple_w_proj[i])
        x = layer_norm(ALPHA * x + ple, ln_g[i, 3], ln_b[i, 3])
    return x
```

```python
import numpy as np
from contextlib import ExitStack

import concourse.bass as bass
import concourse.mybir as mybir
from concourse.bass_utils import run_bass_kernel_spmd

F32 = mybir.dt.float32
BF16 = mybir.dt.bfloat16
AF = mybir.ActivationFunctionType
ALU = mybir.AluOpType

NCORES = 8
D = 2048
DFF = 5504
TOK = 2048
NT = 512
NTILES = TOK // NT
KC = 16
HC = 43
ALPHA = 2.0 ** 0.25
LN_EPS = 1e-5
RMS_EPS = 1e-6
CH = 64
NCH = NT // CH
RING = 3
SLOT = 8192
CCW = 1040

SELF_DEPS = ("act", "dve", "pool")


class Buf:
    __slots__ = ("ap", "names")

    def __init__(self, ap, names):
        self.ap = ap
        self.names = list(names)


def _names(bufs):
    out = []
    for b in bufs:
        if isinstance(b, Buf):
            out.extend(b.names)
        elif isinstance(b, str):
            out.append(b)
        else:
            for bb in b:
                out.extend(bb.names if isinstance(bb, Buf) else [bb])
    return out


class Prog:
    ENGS = ("pe", "act", "dve", "pool", "sp")

    def __init__(self, nc, stack, dry=False):
        self.nc = nc
        self.stack = stack
        self.dry = dry
        self.code = {e: [] for e in self.ENGS}
        self.sem = {}
        self.val = {}
        self.known = {e: {} for e in self.ENGS}
        self.bufs = {}
        self.out_ticks = []

    def getsem(self, key):
        if key not in self.sem:
            self.sem[key] = self.stack.enter_context(self.nc.semaphore(key))
            self.val[key] = 0
        return self.sem[key]

    def _need(self, eng, deps):
        best = {}
        for k, v in deps:
            if v > best.get(k, 0):
                best[k] = v
        own = "e_" + eng
        for k, v in best.items():
            if k == own and eng not in SELF_DEPS:
                continue
            if self.known[eng].get(k, 0) < v:
                self.known[eng][k] = v
                s = self.sem[k]
                self.code[eng].append(lambda e, s=s, v=v: e.wait_ge(s, v))

    def _deps(self, reads, writes):
        deps = []
        for b in reads:
            st = self.bufs.get(b)
            if st and st[0]:
                deps.append(st[0])
        for b in writes:
            st = self.bufs.get(b)
            if st:
                if st[0]:
                    deps.append(st[0])
                deps.extend(st[1])
        return deps

    def _mark(self, reads, writes, tick):
        for b in reads:
            self.bufs.setdefault(b, [None, []])[1].append(tick)
        for b in writes:
            self.bufs[b] = [tick, []]

    def op(self, eng, fn, reads=(), writes=()):
        if self.dry:
            return
        reads = _names(reads)
        writes = _names(writes)
        key = "e_" + eng
        self.getsem(key)
        self._need(eng, self._deps(reads, writes))
        self.val[key] += 1
        tick = (key, self.val[key])
        s = self.sem[key]
        self.code[eng].append(lambda e, fn=fn, s=s: fn(e).then_inc(s, 1))
        self._mark(reads, writes, tick)

    def dma(self, eng, fn, reads=(), writes=(), semkey=None, is_out=False):
        if self.dry:
            return
        reads = _names(reads)
        writes = _names(writes)
        self.getsem(semkey)
        self._need(eng, self._deps(reads, writes))
        self.val[semkey] += 16
        tick = (semkey, self.val[semkey])
        s = self.sem[semkey]
        self.code[eng].append(lambda e, fn=fn, s=s: fn(e).then_inc(s, 16))
        self._mark(reads, writes, tick)
        if is_out:
            self.out_ticks.append(tick)

    def finish(self, eng="sp"):
        if self.dry:
            return
        self._need(eng, self.out_ticks)


class WS:
    def __init__(self, P, ring, units=None):
        self.P = P
        self.ring = ring
        self.units = units
        self.seq = []
        self.ci = 0
        self.issued = 0

    def _issue(self, i):
        src, n = self.units[i]
        s = i % RING
        dst = self.ring[:, s, 0:n]
        self.P.dma("pool", lambda e, dst=dst, src=src: e.dma_start(out=dst, in_=src),
                   reads=(), writes=["ring.%d" % s], semkey="ring%d" % s)
        self.issued = i + 1

    def start(self):
        for i in range(min(RING, len(self.units))):
            self._issue(i)

    def acquire(self, src, n):
        i = self.ci
        if self.units is None:
            self.seq.append((src, n))
            return Buf(self.ring[:, i % RING, 0:n], ["ring.%d" % (i % RING)])
        assert self.units[i][1] == n
        assert i < self.issued
        return Buf(self.ring[:, i % RING, 0:n], ["ring.%d" % (i % RING)])

    def acquire_next(self, src, n):
        i = self.ci + 1
        if self.units is None:
            self.seq.append((src, n))
            return Buf(self.ring[:, i % RING, 0:n], ["ring.%d" % (i % RING)])
        assert self.units[i][1] == n and i < self.issued
        return Buf(self.ring[:, i % RING, 0:n], ["ring.%d" % (i % RING)])

    def release(self):
        i = self.ci
        self.ci += 1
        if self.units is not None and i + RING < len(self.units):
            self._issue(i + RING)


def _record(nc, P, ws, T, mode):
    xr, xb, big, S, Sb, Eall = T["xr"], T["xb"], T["big"], T["S"], T["Sb"], T["Eall"]
    mean_sb, rstd, tmpA, rb, rs = T["mean"], T["rstd"], T["tmpA"], T["rb"], T["rs"]
    kvc, sm, pT, prm, cst = T["kvc"], T["sm"], T["pT"], T["prm"], T["cst"]
    lnab, lbt, oml, identb, onesd, onesv, uprev, uh = (T["lnab"], T["lb"], T["oml"], T["identb"],
                                                       T["onesd"], T["onesv"], T["uprev"], T["uh"])
    psum = T["psum"]
    D_ = T["dram"]

    XOFF = 1 if mode == "F" else 0
    def XR(m):
        return Buf(xr[:, m, :], ["xr.%d" % m])

    def XB(m):
        return Buf(xb[:, m, :], ["xb.%d" % m])

    XB_ALL = ["xb.%d" % m for m in range(KC)]
    XR_ALL = ["xr.%d" % m for m in range(KC)]

    def cells(off, n):
        return ["big.%d" % c for c in range(off // 512, (off + n + 511) // 512)]

    def arena(off, shape, dt):
        n = int(np.prod(shape))
        nb = n * (2 if dt == F32 else 1)
        ap = big[:, off:off + nb]
        if dt == F32:
            ap = ap.bitcast(F32)
        if len(shape) == 2:
            ap = ap.rearrange("p (a b) -> p a b", b=shape[1])
        return Buf(ap, cells(off, nb))

    def sub(buf, idx, off_el_per, dt):
        raise NotImplementedError

    def bank(i):
        return Buf(psum[:, i * 512:(i + 1) * 512], ["ps%d" % i])

    def bank_bf(i):
        return Buf(psum[:, i * 512:(i + 1) * 512].bitcast(BF16), ["ps%d" % i])

    ident = Buf(cst[:, 0:128], ["cst"])
    mask = Buf(cst[0:64, 128:640], ["cst"])
    rmask = Buf(cst[:, 640:1152], ["cst"])
    IDB = Buf(identb[:], ["identb"])
    ONESD = Buf(onesd[:], ["onesd"])
    ONESV = Buf(onesv[:], ["onesv"])

    def pcol(c):
        return prm[:, c:c + 1]

    LNG0, LNB0, CW0, A00, A10, NW0, SEL0 = 0, 64, 128, 152, 160, 168, 169

    O_XIN = 0
    O_H = 0
    O_T = 0
    O_QT, O_KT, O_KHT, O_VT, O_YA, O_YB = 10240, 14336, 18432, 22528, 26624, 30720
    O_MRG = 10240

    def hbuf(j):
        return Buf(big[:, O_H + j * 512:O_H + (j + 1) * 512], ["big.%d" % (O_H // 512 + j)])

    def chunked(off, c):
        return Buf(big[:, off + c * 512:off + (c + 1) * 512], ["big.%d" % (off // 512 + c)])

    def chunked32(off, c):
        o = off + c * 1024
        return Buf(big[:, o:o + 1024].bitcast(F32), cells(o, 1024))

    def pe_multi(groups, reads, writes):
        def fn(e, groups=groups):
            ins = None
            for out_ap, pairs in groups:
                n = len(pairs)
                for i, (l, r) in enumerate(pairs):
                    ins = e.matmul(out_ap, l, r, start=(i == 0), stop=(i == n - 1))
            return ins
        P.op("pe", fn, reads, writes)

    def pe_transposes(items, reads, writes):
        def fn(e, items=items):
            ins = None
            for o, i_, idn in items:
                ins = e.transpose(o, i_, idn)
            return ins
        P.op("pe", fn, reads, writes)

    def act(out, in_, func, bias=0.0, scale=1.0, extra_reads=()):
        P.op("act", lambda e: e.activation(out.ap, in_.ap, func, bias=bias, scale=scale),
             [in_] + list(extra_reads), [out])

    def tt(out, a, b, op, eng="dve"):
        P.op(eng, lambda e: e.tensor_tensor(out.ap, a.ap, b.ap, op), [a, b], [out])

    def ts2(out, a, s1, s2, op0, op1, extra_reads=(), eng="dve"):
        P.op(eng, lambda e: e.tensor_scalar(out.ap, a.ap, s1, s2, op0, op1), [a] + list(extra_reads), [out])

    def ts1(out, a, s1, op, extra_reads=(), eng="dve"):
        P.op(eng, lambda e: e.tensor_single_scalar(out.ap, a.ap, s1, op), [a] + list(extra_reads), [out])

    def stt(out, a, s, b, op0, op1, extra_reads=(), eng="dve"):
        P.op(eng, lambda e: e.scalar_tensor_tensor(out.ap, a.ap, s, b.ap, op0, op1),
             [a, b] + list(extra_reads), [out])

    def cp(out, in_, eng="dve"):
        P.op(eng, lambda e: e.tensor_copy(out.ap, in_.ap), [in_], [out])

    def ln_prep(m):
        i = m % 2
        rbi = Buf(rb[:, i, :], ["rb.%d" % i])
        rsi = Buf(rs[:, i, :], ["rs.%d" % i])
        act(rbi, XR(m), AF.Copy)
        act(rsi, XR(m), AF.Square)

    def ln_mm(m):
        i = m % 2
        rbi = Buf(rb[:, i, :], ["rb.%d" % i])
        rsi = Buf(rs[:, i, :], ["rs.%d" % i])
        b6, b7 = bank(6), bank(7)

        def fn(e, m=m):
            e.matmul(b6.ap, onesd[:], rbi.ap, start=(m == 0), stop=(m == KC - 1), skip_group_check=True)
            return e.matmul(b7.ap, onesd[:], rsi.ap, start=(m == 0), stop=(m == KC - 1), skip_group_check=True)
        P.op("pe", fn, [rbi, rsi, ONESD], [b6, b7])

    def ln_step(m):
        if m >= 1:
            ln_mm(m - 1)

    def pipelined_groups(specs):
        ws_ = [(sl.ap.rearrange("p (k c) -> p k c", c=512), bk, cs) for sl, bk, cs in specs]
        slots = []
        for sl, _, _ in specs:
            if sl not in slots:
                slots.append(sl)
        for k in range(KC):
            def fn(e, k=k):
                ins = None
                for w, bk, cs in ws_:
                    ins = e.matmul(bk.ap, w[:, k, cs], xb[:, k, :], start=(k == 0), stop=(k == KC - 1),
                                   skip_group_check=True)
                return ins
            P.op("pe", fn, slots + ["xb.%d" % k], [bk for _, bk, _ in specs])

    def ln_apply(l, final=False, cast_next=False):
        MEAN = Buf(mean_sb[:], ["mean"])
        RSTD = Buf(rstd[:], ["rstd"])
        act(MEAN, bank(6), AF.Copy)
        tt(RSTD, MEAN, MEAN, ALU.mult)
        tt(RSTD, bank(7), RSTD, ALU.subtract)
        ts1(RSTD, RSTD, LN_EPS, ALU.add)
        act(RSTD, RSTD, AF.Ln)
        act(RSTD, RSTD, AF.Exp, scale=-0.5)
        for m in range(KC):
            i = m % 2
            t = Buf(tmpA[:, i, :], ["tmpA.%d" % i])
            tt(t, XR(m), MEAN, ALU.subtract)
            stt(t, t, pcol(LNG0 + l * 16 + m), RSTD, ALU.mult, ALU.mult, extra_reads=["prm"])
            if not final:
                act(XB(m), t, AF.Identity, bias=pcol(LNB0 + l * 16 + m), scale=1.0, extra_reads=["prm"])
                act(XR(m), t, AF.Identity, bias=lnab[:, l * 16 + m:l * 16 + m + 1], scale=ALPHA,
                    extra_reads=["lnab"])
            else:
                act(XR(m), t, AF.Identity, bias=pcol(LNB0 + l * 16 + m), scale=1.0, extra_reads=["prm"])
                if cast_next:
                    load_x_cast(m)

    def ffn(f, l, pipelined=False, tail=None):
        wfi, wfo = D_["wfi%d" % f], D_["wfo%d" % f]

        def evac(j, pa, pu):
            sa = Buf(tmpA[:, j % 2, :], ["tmpA.%d" % (j % 2)])
            act(sa, pa, AF.Silu)
            tt(hbuf(j), sa, pu, ALU.mult)

        u0 = 0
        if pipelined:
            s0 = ws.acquire(wfi[0], 8192)
            s1 = ws.acquire_next(wfi[1], 8192)
            specs = []
            for uu, sl in ((0, s0), (1, s1)):
                for jj in range(2):
                    b0 = uu * 4 + jj * 2
                    specs.append((sl, bank(b0), slice(jj * 128, (jj + 1) * 128)))
                    specs.append((sl, bank(b0 + 1), slice(256 + jj * 128, 256 + (jj + 1) * 128)))
            pipelined_groups(specs)
            for j in range(4):
                evac(j, bank(2 * j), bank(2 * j + 1))
            ws.release()
            ws.release()
            u0 = 2
        for u in range(u0, 22):
            slot = ws.acquire(wfi[u], 8192)
            w = slot.ap.rearrange("p (k c) -> p k c", c=512)
            for jj in range(2):
                j = 2 * u + jj
                if j >= HC:
                    break
                pa = bank(0 if j % 2 == 0 else 2)
                pu = bank(1 if j % 2 == 0 else 3)
                groups = [
                    (pa.ap, [(w[:, k, jj * 128:(jj + 1) * 128], xb[:, k, :]) for k in range(KC)]),
                    (pu.ap, [(w[:, k, 256 + jj * 128:256 + (jj + 1) * 128], xb[:, k, :]) for k in range(KC)]),
                ]
                pe_multi(groups, [slot] + XB_ALL, [pa, pu])
                evac(j, pa, pu)
            ws.release()
        hall = ["big.%d" % (O_H // 512 + j) for j in range(HC)]
        for m in range(KC):
            slot = ws.acquire(wfo[m], DFF)
            w = slot.ap.rearrange("p (k c) -> p k c", c=128)
            po = bank(4 + m % 2)
            pe_multi([(po.ap, [(w[:, k, :], hbuf(k).ap) for k in range(HC)])], [slot] + hall, [po])
            ws.release()
            ln_step(m)
            stt(XR(m), po, 0.5, XR(m), ALU.mult, ALU.add)
            ln_prep(m)
        if tail is not None:
            tail()
        ln_mm(KC - 1)
        ln_apply(l)

    O_XIN2 = 16384

    def xin_bufs():
        return [Buf(big[:, O_XIN2 + b * 4096:O_XIN2 + (b + 1) * 4096].bitcast(F32), cells(O_XIN2 + b * 4096, 4096))
                for b in range(4)]

    def xin_fm():
        return [Buf(big[:, O_XIN2 + g * 4096:O_XIN2 + (g + 1) * 4096].bitcast(F32).rearrange("p (m j) -> p m j", j=NT),
                    cells(O_XIN2 + g * 4096, 4096)) for g in range(4)]

    def load_x_dma(t):
        x_d = D_["x"]
        t = t + XOFF
        xin = xin_fm()
        for g in range(4):
            src = x_d[t][:, g * 4:(g + 1) * 4, :]
            P.dma("sp", lambda e, o=xin[g].ap, s=src: e.dma_start(out=o, in_=s), [], [xin[g]], semkey="xin%d" % g)

    def xin_chunk(m):
        xin = xin_fm()
        return Buf(xin[m // 4].ap[:, m % 4, :], xin[m // 4].names)

    def load_x_cast(m):
        act(XB(m), xin_chunk(m), AF.Copy)

    def load_x_scale():
        for m in range(KC):
            ts1(XR(m), xin_chunk(m), ALPHA, ALU.mult)

    def load_x_tr():
        for m in range(KC):
            load_x_cast(m)
        load_x_scale()

    def load_x(t):
        load_x_dma(t)
        load_x_tr()

    def store_y(t):
        y_d = D_["y"]
        for g in range(4):
            P.dma("sp", lambda e, g=g: e.dma_start(out=y_d[t][:, g * 4:(g + 1) * 4, :], in_=xr[:, g * 4:(g + 1) * 4, :]),
                  ["xr.%d" % (g * 4 + mm) for mm in range(4)], [], semkey="yout%d" % g, is_out=True)

    TMP = [chunked32(O_T + 4096, i) for i in range(6)]
    EC = [chunked32(O_T, i) for i in range(4)]

    def QT(c):
        return chunked(O_QT, c)

    def KT(c):
        return chunked(O_KT, c)

    def KHT(c):
        return chunked(O_KHT, c)

    def VT(c):
        return chunked(O_VT, c)

    def zgroup(slot, jj, bk):
        w = slot.ap.rearrange("p (k c) -> p k c", c=512)
        pe_multi([(bk.ap, [(w[:, k, jj * 128:(jj + 1) * 128], xb[:, k, :]) for k in range(KC)])],
                 [slot] + XB_ALL, [bk])

    def f_head(c, slot, jj, full):
        bk = bank(c % 4)
        zgroup(slot, jj, bk)
        T1, T2, T3, T4, T5, T6 = TMP
        ecc = EC[c % 4] if full else T4
        act(T1, bk, AF.Sigmoid)
        ts2(T1, T1, oml[:, c:c + 1], lbt[:, c:c + 1], ALU.mult, ALU.add, extra_reads=["lb"])
        act(T2, T1, AF.Ln)
        P.op("dve", lambda e: e.tensor_tensor_scan(T3.ap, rmask.ap, T2.ap, 0.0, ALU.mult, ALU.add),
             [rmask, T2], [T3])
        act(ecc, T3, AF.Exp)
        act(T5, T3, AF.Exp, scale=-1.0)
        ts2(T1, T1, -1.0, 1.0, ALU.mult, ALU.add)
        tt(T2, T1, T5, ALU.mult)
        if full:
            act(KT(c), T2, AF.Copy)
        ec3 = ecc.ap.rearrange("p (c j) -> p c j", j=CH)
        ends = ec3[:, :, CH - 1:CH]
        cp(Buf(Eall[:, c, :], ["Eall"]), Buf(ec3[:, :, CH - 1], ecc.names))
        k3 = Buf(T2.ap.rearrange("p (c j) -> p c j", j=CH), T2.names)
        o3 = Buf(KHT(c).ap.rearrange("p (c j) -> p c j", j=CH), KHT(c).names)
        tt(o3, k3, Buf(ends.to_broadcast([128, NCH, CH]), ecc.names), ALU.mult)

    def f_unit(hh, slot, full):
        A = TMP[0:4]
        Bf = EC
        X, Y = TMP[4], TMP[5]
        for jj in range(4):
            c = hh * 4 + jj
            bk = bank(jj)
            zgroup(slot, jj, bk)
            act(A[jj], bk, AF.Sigmoid)
            ts2(A[jj], A[jj], oml[:, c:c + 1], lbt[:, c:c + 1], ALU.mult, ALU.add, extra_reads=["lb"])
        for jj in range(4):
            act(Bf[jj], A[jj], AF.Ln)
            ts2(A[jj], A[jj], -1.0, 1.0, ALU.mult, ALU.add)
        XS = [TMP[4], Buf(tmpA[:, 0, :], ["tmpA.0"])]
        YS = [TMP[5], Buf(tmpA[:, 1, :], ["tmpA.1"])]
        for jj in range(4):
            c = hh * 4 + jj
            X, Y = XS[jj % 2], YS[jj % 2]
            P.op("dve", lambda e, jj=jj, X=X: e.tensor_tensor_scan(X.ap, rmask.ap, Bf[jj].ap, 0.0, ALU.mult, ALU.add),
                 [rmask, Bf[jj]], [X])
            act(Bf[jj], X, AF.Exp)
            act(Y, X, AF.Exp, scale=-1.0)
            tt(A[jj], A[jj], Y, ALU.mult)
            if full:
                act(KT(c), A[jj], AF.Copy)
            ec3 = Bf[jj].ap.rearrange("p (c j) -> p c j", j=CH)
            ends = ec3[:, :, CH - 1:CH]
            cp(Buf(Eall[:, c, :], ["Eall"]), Buf(ec3[:, :, CH - 1], Bf[jj].names))
            k3 = Buf(A[jj].ap.rearrange("p (c j) -> p c j", j=CH), A[jj].names)
            o3 = Buf(KHT(c).ap.rearrange("p (c j) -> p c j", j=CH), KHT(c).names)
            tt(o3, k3, Buf(ends.to_broadcast([128, NCH, CH]), Bf[jj].names), ALU.mult)

    def q_head(c, slot, jj):
        bk = bank(c % 4)
        zgroup(slot, jj, bk)
        T6 = TMP[5]
        act(T6, bk, AF.Silu)
        tt(QT(c), T6, EC[c % 4], ALU.mult)

    def v_head(c, slot, jj, k):
        bk = bank(c % 4)
        zgroup(slot, jj, bk)
        if k % 2 == 0:
            act(VT(c), bk, AF.Copy)
        else:
            cp(VT(c), bk)

    O_O = O_T

    def scan(full):
        KHT_ALL = cells(O_KHT, 4096)
        VT_ALL = cells(O_VT, 4096)
        QT_ALL = cells(O_QT, 4096)
        KT_ALL = cells(O_KT, 4096)
        O_ALL = cells(O_O, 8192)
        o3 = big[:, O_O:O_O + 8192].bitcast(F32).rearrange("p (c t) -> p c t", t=NT)
        SB_ = Buf(S[:], ["S"])
        for cc in range(NCH):
            sl = slice(cc * CH, (cc + 1) * CH)
            i = cc % 2
            ptk, ptv = bank_bf(0), bank_bf(1)
            items = [(ptk.ap[0:CH, c * 128:(c + 1) * 128], big[:, O_KHT + c * 512 + cc * CH:O_KHT + c * 512 + (cc + 1) * CH],
                      identb[:]) for c in range(8)]
            items += [(ptv.ap[0:CH, c * 128:(c + 1) * 128], big[:, O_VT + c * 512 + cc * CH:O_VT + c * 512 + (cc + 1) * CH],
                       identb[:]) for c in range(8)]
            pe_transposes(items, KHT_ALL + VT_ALL + [IDB], [ptk, ptv])
            khc = Buf(kvc[0:CH, i, :], ["kvc.%d" % i])
            vc = Buf(kvc[0:CH, 2 + i, :], ["kvc.%d" % (2 + i)])
            act(khc, Buf(ptk.ap[0:CH, :], ptk.names), AF.Copy)
            cp(vc, Buf(ptv.ap[0:CH, :], ptv.names))
            if full:
                pS = bank(2)
                groups = [(pS.ap[0:CH, c * CH:(c + 1) * CH],
                           [(big[:, O_KT + c * 512 + cc * CH:O_KT + c * 512 + (cc + 1) * CH],
                             big[:, O_QT + c * 512 + cc * CH:O_QT + c * 512 + (cc + 1) * CH])]) for c in range(8)]
                pe_multi(groups, KT_ALL + QT_ALL, [pS])
                smi = Buf(sm[0:CH, i, :], ["sm.%d" % i])
                tt(smi, Buf(pS.ap[0:CH, :], pS.names), mask, ALU.mult)
                pO = bank(3)
                cur = cc % 2
                groups = []
                for c in range(8):
                    groups.append((pO.ap[:, c * CH:(c + 1) * CH],
                                   [(Sb[:, cur, c * 128:(c + 1) * 128],
                                     big[:, O_QT + c * 512 + cc * CH:O_QT + c * 512 + (cc + 1) * CH]),
                                    (vc.ap[:, c * 128:(c + 1) * 128], smi.ap[:, c * CH:(c + 1) * CH])]))
                pe_multi(groups, ["Sb.%d" % cur, vc, smi] + QT_ALL, [pO])
                act(Buf(o3[:, :, sl], O_ALL), Buf(pO.ap.rearrange("p (c t) -> p c t", t=CH), pO.names), AF.Copy)
            pU = Buf(psum[:, 4 * 512:6 * 512], ["ps4", "ps5"])
            groups = [(pU.ap[:, c * 128:(c + 1) * 128],
                       [(khc.ap[:, c * 128:(c + 1) * 128], vc.ap[:, c * 128:(c + 1) * 128])]) for c in range(8)]
            pe_multi(groups, [khc, vc], [pU])
            S3 = Buf(S[:].rearrange("p (c v) -> p c v", v=128), ["S"])
            eb = Buf(Eall[:, :, cc:cc + 1].to_broadcast([128, 8, 128]), ["Eall"])
            tt(S3, S3, eb, ALU.mult)
            tt(SB_, SB_, pU, ALU.add)
            if full:
                nxt = (cc + 1) % 2
                act(Buf(Sb[:, nxt, :], ["Sb.%d" % nxt]), SB_, AF.Copy)

    def mixer_a(halo):
        wmin = D_["wmin"]
        for hh in range(2):
            slot = ws.acquire(wmin[8 + hh], 8192)
            f_unit(hh, slot, full=False)
            ws.release()
        k = 0
        for hh in range(2):
            slot = ws.acquire(wmin[10 + hh], 8192)
            for jj in range(4):
                v_head(hh * 4 + jj, slot, jj, k)
                k += 1
            ws.release()
        scan(full=False)
        if halo:
            cs = chunked32(O_T + 4096, 0)
            c3 = Buf(cs.ap[:, 0:16].rearrange("p (c t) -> p c t", t=2), cs.names)
            for which in range(2):
                for hh in range(2):
                    slot = ws.acquire(wmin[2 + 2 * which + hh], 8192)
                    w = slot.ap.rearrange("p (k c) -> p k c", c=512)
                    bk = bank(hh)
                    groups = [(bk.ap[:, jj * 2:jj * 2 + 2],
                               [(w[:, k, jj * 128:(jj + 1) * 128], xb[:, k, NT - 2:NT]) for k in range(KC)])
                              for jj in range(4)]
                    pe_multi(groups, [slot] + XB_ALL, [bk])
                    ws.release()
                    src = Buf(bk.ap[:, 0:8].rearrange("p (c t) -> p c t", t=2), bk.names)
                    dst_c = Buf(c3.ap[:, hh * 4:(hh + 1) * 4, :], c3.names)
                    dst_u = Buf(uh[:, hh * 8:(hh + 1) * 8].rearrange("p (c t) -> p c t", t=2), ["uh"])
                    if which == 0:
                        cp(dst_c, src)
                    else:
                        tt(dst_u, src, dst_c, ALU.mult)

    def mixer_b(t):
        wmin, wbr, wmo = D_["wmin"], D_["wbr"], D_["wmo"]
        for hf in range(2):
            Cs = [chunked32(O_T, q) for q in range(4)]
            ubuf = [Buf(big[:, O_T + 4096 + q * 1032:O_T + 4096 + (q + 1) * 1032].bitcast(F32),
                        cells(O_T + 4096 + q * 1032, 1032)) for q in range(4)]
            slot = ws.acquire(wmin[2 + hf], 8192)
            if hf == 0:
                slot_h = ws.acquire_next(wmin[4 + hf], 8192)
                pipelined_groups([(slot, bank(q), slice(q * 128, (q + 1) * 128)) for q in range(4)] +
                                 [(slot_h, bank(4 + q), slice(q * 128, (q + 1) * 128)) for q in range(4)])
            for q in range(4):
                bk = bank(q)
                if hf != 0:
                    zgroup(slot, q, bk)
                act(Cs[q], bk, AF.Copy)
            ws.release()
            if hf != 0:
                slot_h = ws.acquire(wmin[4 + hf], 8192)
            for q in range(4):
                c = hf * 4 + q
                bk = bank(4 + q) if hf == 0 else bank(q)
                if hf != 0:
                    zgroup(slot_h, q, bk)
                ub = ubuf[q]
                tt(Buf(ub.ap[:, 2:2 + NT], ub.names), bk, Cs[q], ALU.mult)
                act(Buf(ub.ap[:, 0:2], ub.names), Buf(uprev[:, c * 2:c * 2 + 2], ["uprev"]), AF.Copy)
                ts1(Cs[q], Buf(ub.ap[:, 0:NT], ub.names), pcol(CW0 + 0 * 8 + c), ALU.mult, extra_reads=["prm"])
                stt(Cs[q], Buf(ub.ap[:, 1:1 + NT], ub.names), pcol(CW0 + 1 * 8 + c), Cs[q], ALU.mult, ALU.add,
                    extra_reads=["prm"])
                stt(Cs[q], Buf(ub.ap[:, 2:2 + NT], ub.names), pcol(CW0 + 2 * 8 + c), Cs[q], ALU.mult, ALU.add,
                    extra_reads=["prm"])
                act(Buf(uprev[:, c * 2:c * 2 + 2], ["uprev"]), Buf(ub.ap[:, NT:NT + 2], ub.names), AF.Copy)
            ws.release()
            slot = ws.acquire(wmin[0 + hf], 8192)
            for q in range(4):
                c = hf * 4 + q
                bk = bank(q)
                zgroup(slot, q, bk)
                tt(chunked(O_YA, c), bk, Cs[q], ALU.mult)
            ws.release()
        for hh in range(2):
            slot = ws.acquire(wmin[8 + hh], 8192)
            f_unit(hh, slot, full=True)
            ws.release()
            slot = ws.acquire(wmin[6 + hh], 8192)
            for jj in range(4):
                q_head(hh * 4 + jj, slot, jj)
            ws.release()
        k = 0
        for hh in range(2):
            slot = ws.acquire(wmin[10 + hh], 8192)
            for jj in range(4):
                v_head(hh * 4 + jj, slot, jj, k)
                k += 1
            ws.release()
        scan(full=True)
        ple_prep(t)
        for hh in range(2):
            slot = ws.acquire(wmin[12 + hh], 8192)
            for jj in range(4):
                c = hh * 4 + jj
                oc = chunked32(O_O, c)
                i = c % 2
                osq = Buf(rb[:, i, :], ["rb.%d" % i])
                act(osq, oc, AF.Square)
                bm = bank(6 + i)
                pe_multi([(bm.ap, [(onesv[:], osq.ap)])], [osq, ONESV], [bm])
                rsq = chunked32(O_O + 8192, 0)
                ts1(rsq, bm, RMS_EPS, ALU.add)
                act(rsq, rsq, AF.Ln)
                act(rsq, rsq, AF.Exp, scale=-0.5)
                stt(oc, oc, pcol(NW0), rsq, ALU.mult, ALU.mult, extra_reads=["prm"])
                bk = bank(jj)
                zgroup(slot, jj, bk)
                sg = chunked32(O_O + 9216, 0)
                act(sg, bk, AF.Silu)
                tt(chunked(O_YB, c), oc, sg, ALU.mult)
            ws.release()
        YA_ALL = cells(O_YA, 4096)
        YB_ALL = cells(O_YB, 4096)
        for mg in range(4):
            sgt = [chunked32(O_T, q) for q in range(4)]
            mgt = [chunked32(O_T + 4096, q) for q in range(4)]
            for br in range(2):
                slot = ws.acquire(wmin[14 + 4 * br + mg], 8192)
                for q in range(4):
                    bk = bank(q)
                    zgroup(slot, q, bk)
                    act(sgt[q], bk, AF.Sigmoid)
                ws.release()
                slot = ws.acquire(wbr[mg * 2 + br], 4096)
                w = slot.ap.rearrange("p (k c) -> p k c", c=512)
                yoff = O_YA if br == 0 else O_YB
                for q in range(4):
                    m = mg * 4 + q
                    bk = bank(4 + q % 2)
                    pe_multi([(bk.ap, [(w[:, k, q * 128:(q + 1) * 128], big[:, yoff + k * 512:yoff + (k + 1) * 512])
                                       for k in range(8)])], [slot] + (YA_ALL if br == 0 else YB_ALL), [bk])
                    if br == 0:
                        tt(mgt[q], sgt[q], bk, ALU.mult)
                    else:
                        tt(sgt[q], sgt[q], bk, ALU.mult)
                        tt(chunked(O_MRG, m), sgt[q], mgt[q], ALU.add)
                ws.release()
        MRG_ALL = cells(O_MRG, 8192)
        for mq in range(4):
            slot = ws.acquire(wmo[mq], 8192)
            w = slot.ap.rearrange("p (k c) -> p k c", c=512)
            for q in range(4):
                m = mq * 4 + q
                bk = bank(4 + m % 2)
                pe_multi([(bk.ap, [(w[:, k, q * 128:(q + 1) * 128], big[:, O_MRG + k * 512:O_MRG + (k + 1) * 512])
                                   for k in range(KC)])], [slot] + MRG_ALL, [bk])
                ln_step(m)
                tt(XR(m), XR(m), bk, ALU.add)
                ln_prep(m)
            ws.release()
        ln_mm(KC - 1)
        ln_apply(1)

    O_PP = 16384

    def ple_prep(t):
        p_d = D_["p"]
        pin = Buf(kvc[:, 0:2, :].bitcast(F32), ["kvc.0", "kvc.1"])
        P.dma("sp", lambda e: e.dma_start(out=pin.ap, in_=p_d[t]), [], [pin], semkey="pin")
        for kc in range(2):
            act(Buf(pT[:, kc, :], ["pT"]), Buf(pin.ap[:, kc, :], pin.names), AF.Copy)

    def ple_pp():
        wpp = D_["wpp"]
        pslot = ws.acquire(wpp[0], 4096)
        wp = pslot.ap.rearrange("p (k c) -> p k c", c=2048)
        for m in range(KC):
            bb = bank(m % 4)
            pe_multi([(bb.ap, [(wp[:, k, m * 128:(m + 1) * 128], pT[:, k, :]) for k in range(2)])],
                     [pslot, "pT"], [bb])
            if m % 2 == 0:
                act(chunked32(O_PP, m), bb, AF.Copy)
            else:
                cp(chunked32(O_PP, m), bb)
        ws.release()

    def ple(t):
        wpg = D_["wpg"]
        for mq in range(4):
            slot = ws.acquire(wpg[mq], 8192)
            w = slot.ap.rearrange("p (k c) -> p k c", c=512)
            for q in range(4):
                m = mq * 4 + q
                ba = bank(m % 4)
                if mq == 0:
                    if q == 0:
                        pipelined_groups([(slot, bank(qq), slice(qq * 128, (qq + 1) * 128)) for qq in range(4)])
                else:
                    pe_multi([(ba.ap, [(w[:, k, q * 128:(q + 1) * 128], xb[:, k, :]) for k in range(KC)])],
                             [slot] + XB_ALL, [ba])
                ln_step(m)
                tg = Buf(tmpA[:, m % 2, :], ["tmpA.%d" % (m % 2)])
                act(tg, ba, AF.Sigmoid)
                tt(tg, tg, chunked32(O_PP, m), ALU.mult)
                tt(XR(m), XR(m), tg, ALU.add)
                ln_prep(m)
            ws.release()
        ln_mm(KC - 1)
        if t + 1 < NTILES:
            load_x_dma(t + 1)
        ln_apply(3, final=True, cast_next=(t + 1 < NTILES))

    P.dma("sp", lambda e: e.dma_start(out=prm[:], in_=D_["prm"]), [], ["prm"], semkey="prm")
    P.dma("sp", lambda e: e.dma_start(out=cst[:], in_=D_["cst"]), [], ["cst"], semkey="cst")
    P.op("dve", lambda e: e.tensor_single_scalar(lnab[:], prm[:, LNB0:LNB0 + 64], ALPHA, ALU.mult), ["prm"], ["lnab"])
    P.op("dve", lambda e: e.tensor_tensor(lbt[:], prm[:, A00:A00 + 8], prm[:, A10:A10 + 8], ALU.subtract), ["prm"], ["lb"])
    P.op("act", lambda e: e.activation(lbt[:], lbt[:], AF.Sigmoid), ["lb"], ["lb"])
    P.op("dve", lambda e: e.tensor_scalar(oml[:], lbt[:], -1.0, 1.0, ALU.mult, ALU.add), ["lb"], ["lb"])
    P.op("act", lambda e: e.activation(identb[:], cst[:, 0:128], AF.Copy), ["cst"], ["identb"])
    P.op("dve", lambda e: e.memset(onesd[:], 1.0 / D), [], ["onesd"])
    P.op("dve", lambda e: e.memset(onesv[:], 1.0 / 128.0), [], ["onesv"])
    P.op("dve", lambda e: e.memset(S[:], 0.0), [], ["S"])
    P.op("dve", lambda e: e.memset(uh[:], 0.0), [], ["uh"])
    if ws.units is not None:
        ws.start()

    if mode == "F":
        load_x(-1)
        ffn(1, 0, pipelined=True)
        mixer_a(True)
        SBF = Buf(S[:], ["S"])
        ts1(SBF, SBF, pcol(SEL0), ALU.mult, extra_reads=["prm"])
        ts1(Buf(uprev[:], ["uprev"]), Buf(uh[:], ["uh"]), pcol(SEL0), ALU.mult, extra_reads=["prm"])
        P.op("act", lambda e: e.activation(Sb[:, 0, :], S[:], AF.Copy), ["S"], ["Sb.0"])
        load_x_dma(0)
        for t in range(NTILES):
            if t == 0:
                load_x_tr()
            else:
                load_x_scale()
            ffn(1, 0, pipelined=True)
            mixer_b(t)
            ffn(2, 2, pipelined=True, tail=ple_pp)
            ple(t)
            store_y(t)
        P.finish("sp")
        return

    if mode in ("A", "AB"):
        for t in range(NTILES):
            load_x(t)
            ffn(1, 0)
            P.dma("sp", lambda e, t=t: e.dma_start(out=D_["xs"][t], in_=xr[:].rearrange("p a b -> p (a b)")),
                  XR_ALL, ["xs.%d" % t], semkey="spill", is_out=(mode == "A"))
            mixer_a(t == NTILES - 1)

    SB_ = Buf(S[:], ["S"])
    UP = Buf(uprev[:], ["uprev"])
    if mode == "A":
        ccin = D_["ccin"]
        P.dma("sp", lambda e: e.dma_start(out=ccin[:, 0:1024], in_=S[:]), ["S"], ["ccin"], semkey="ccs", is_out=True)
        P.dma("sp", lambda e: e.dma_start(out=ccin[:, 1024:CCW], in_=uh[:]), ["uh"], ["ccin"], semkey="ccs", is_out=True)
    elif mode == "B":
        gin = D_["gin"]
        P.dma("sp", lambda e: e.dma_start(out=S[:], in_=gin[:, 0:1024]), [], ["S"], semkey="gin")
        P.dma("sp", lambda e: e.dma_start(out=uprev[:], in_=gin[:, 1024:CCW]), [], ["uprev"], semkey="gin")
    else:
        ccin, ccout = D_["ccin"], D_["ccout"]
        P.dma("sp", lambda e: e.dma_start(out=ccin[:, 0:1024], in_=S[:]), ["S"], ["ccin"], semkey="ccs")
        P.dma("sp", lambda e: e.dma_start(out=ccin[:, 1024:CCW], in_=uh[:]), ["uh"], ["ccin"], semkey="ccs")
        P.dma("pool", lambda e: e.collective_compute("AllGather", ALU.bypass, replica_groups=[list(range(NCORES))],
                                                     ins=[ccin], outs=[ccout]),
              ["ccin"], ["ccout"], semkey="cc")
        G = arena(0, [NCORES, CCW], F32)
        P.dma("sp", lambda e: e.dma_start(out=G.ap, in_=ccout.rearrange("(r p) f -> p r f", p=128)),
              ["ccout"], [G], semkey="ccg")
        for r in range(NCORES):
            gs = Buf(G.ap[:, r, 0:1024], G.names)
            gu = Buf(G.ap[:, r, 1024:CCW], G.names)
            if r == 0:
                ts1(SB_, gs, pcol(SEL0 + r), ALU.mult, extra_reads=["prm"])
                ts1(UP, gu, pcol(SEL0 + r), ALU.mult, extra_reads=["prm"])
            else:
                stt(SB_, gs, pcol(SEL0 + r), SB_, ALU.mult, ALU.add, extra_reads=["prm"])
                stt(UP, gu, pcol(SEL0 + r), UP, ALU.mult, ALU.add, extra_reads=["prm"])

    if mode in ("B", "AB"):
        P.op("act", lambda e: e.activation(Sb[:, 0, :], S[:], AF.Copy), ["S"], ["Sb.0"])
        for t in range(NTILES):
            P.dma("sp", lambda e, t=t: e.dma_start(out=xr[:].rearrange("p a b -> p (a b)"), in_=D_["xs"][t]),
                  ["xs.%d" % t], XR_ALL, semkey="reload")
            for m in range(KC):
                act(XB(m), XR(m), AF.Copy, scale=1.0 / ALPHA)
            mixer_b(t)
            ffn(2, 2)
            ple(t)
            store_y(t)
    P.finish("sp")


def _ple_fix_note():
    pass


def build(mode):
    nc = bass.Bass("TRN2", target_bir_lowering=False)
    dram = {}

    def din(name, shape):
        dram[name] = nc.dram_tensor(name, shape, F32, kind="ExternalInput").ap()

    if mode in ("A", "AB"):
        din("x", [TOK, D])
    if mode == "F":
        din("x", [NTILES + 1, 128, KC, NT])
    din("prm", [128, 192])
    din("cst", [128, 1152])
    if mode in ("A", "AB", "F"):
        din("wfi1", [22, 128, 8192])
        din("wfo1", [16, 128, DFF])
    din("wmin", [22, 128, 8192])
    if mode in ("B", "AB", "F"):
        din("p", [NTILES, 128, 2, NT])
        din("wfi2", [22, 128, 8192])
        din("wfo2", [16, 128, DFF])
        din("wbr", [8, 128, 4096])
        din("wmo", [4, 128, 8192])
        din("wpg", [4, 128, 8192])
        din("wpp", [1, 128, 4096])
        dram["y"] = nc.dram_tensor("y", [NTILES, 128, KC, NT], F32, kind="ExternalOutput").ap()
    if mode == "A":
        dram["xs"] = nc.dram_tensor("xs", [NTILES, 128, 8192], F32, kind="ExternalOutput").ap()
        dram["ccin"] = nc.dram_tensor("ccin", [128, CCW], F32, kind="ExternalOutput").ap()
    elif mode == "B":
        din("xs", [NTILES, 128, 8192])
        din("gin", [128, CCW])
    elif mode == "AB":
        dram["xs"] = nc.dram_tensor("xs", [NTILES, 128, 8192], F32).ap()
        dram["ccin"] = nc.dram_tensor("ccin", [128, CCW], F32).ap()
        dram["ccout"] = nc.dram_tensor("ccout", [NCORES * 128, CCW], F32).ap()

    with ExitStack() as st:
        def sb(name, shape, dt):
            return st.enter_context(nc.sbuf_tensor("s_" + name, shape, dt))

        T = {"dram": dram}
        T["xr"] = sb("xr", [128, KC, NT], F32)
        T["xb"] = sb("xb", [128, KC, NT], BF16)
        T["big"] = sb("big", [128, 34816], BF16)
        ring = sb("ring", [128, RING, SLOT], BF16)
        T["S"] = sb("S", [128, 1024], F32)
        T["Sb"] = sb("Sb", [128, 2, 1024], BF16)
        T["Eall"] = sb("Eall", [128, 8, NCH], F32)
        T["mean"] = sb("mean", [128, NT], F32)
        T["rstd"] = sb("rstd", [128, NT], F32)
        T["tmpA"] = sb("tmpA", [128, 2, NT], F32)
        T["rb"] = sb("rb", [128, 2, NT], BF16)
        T["rs"] = sb("rs", [128, 2, NT], BF16)
        T["kvc"] = sb("kvc", [128, 4, 1024], BF16)
        T["sm"] = sb("sm", [128, 2, NT], BF16)
        T["pT"] = sb("pT", [128, 2, NT], BF16)
        T["prm"] = sb("prm", [128, 192], F32)
        T["cst"] = sb("cst", [128, 1152], F32)
        T["lnab"] = sb("lnab", [128, 64], F32)
        T["lb"] = sb("lb", [128, 8], F32)
        T["oml"] = sb("oml", [128, 8], F32)
        T["identb"] = sb("identb", [128, 128], BF16)
        T["onesd"] = sb("onesd", [128, 128], BF16)
        T["onesv"] = sb("onesv", [128, 128], BF16)
        T["uprev"] = sb("uprev", [128, 16], F32)
        T["uh"] = sb("uh", [128, 16], F32)
        T["psum"] = st.enter_context(nc.psum_tensor("psum", [128, 4096], F32))

        Pd = Prog(nc, st, dry=True)
        wsd = WS(Pd, ring, None)
        _record(nc, Pd, wsd, T, mode)
        units = wsd.seq
        P = Prog(nc, st, dry=False)
        ws = WS(P, ring, units)
        _record(nc, P, ws, T, mode)
        assert ws.ci == len(units), (ws.ci, len(units))

        with nc.Block() as block:
            @block.tensor
            def _(e):
                for f in P.code["pe"]:
                    f(e)

            @block.scalar
            def _(e):
                for f in P.code["act"]:
                    f(e)

            @block.vector
            def _(e):
                for f in P.code["dve"]:
                    f(e)

            @block.gpsimd
            def _(e):
                for f in P.code["pool"]:
                    f(e)

            @block.sync
            def _(e):
                for f in P.code["sp"]:
                    f(e)
    return nc


def _k_units(W, ncol):
    K, C = W.shape
    kc = K // 128
    u = C // ncol
    a = W.reshape(kc, 128, u, ncol).transpose(2, 1, 0, 3)
    return np.ascontiguousarray(a).reshape(u, 128, kc * ncol)


def _prep_shared(inp):
    out = {}
    for f, (wi, wo) in ((1, ("ffn1_w_in", "ffn1_w_out")), (2, ("ffn2_w_in", "ffn2_w_out"))):
        W = inp[wi][0]
        Wp = np.zeros((D, 2, 22 * 256), np.float32)
        Wp[:, 0, :DFF] = W[:, :DFF]
        Wp[:, 1, :DFF] = W[:, DFF:]
        a = Wp.reshape(KC, 128, 2, 22, 256).transpose(3, 1, 0, 2, 4)
        out["wfi%d" % f] = np.ascontiguousarray(a).reshape(22, 128, 8192)
        out["wfo%d" % f] = _k_units(inp[wo][0], 128)
    out["wmin"] = _k_units(inp["mix_w_in"][0], 512)
    wa = _k_units(inp["branch_w_conv"][0], 512)
    wb = _k_units(inp["branch_w_hgrn"][0], 512)
    out["wbr"] = np.ascontiguousarray(np.stack([wa, wb], axis=1)).reshape(8, 128, 4096)
    out["wmo"] = _k_units(inp["mix_w_out"][0], 512)
    out["wpg"] = _k_units(inp["ple_w_gate"][0], 512)
    out["wpp"] = _k_units(inp["ple_w_proj"][0], 2048)
    cst = np.zeros((128, 1152), np.float32)
    cst[:, 0:128] = np.eye(128, dtype=np.float32)
    s = np.arange(CH)[:, None]
    tq = np.arange(CH)[None, :]
    m = (s <= tq).astype(np.float32)
    cst[0:CH, 128:640] = np.tile(m, (1, 8))
    rm = np.ones(NT, np.float32)
    rm[::CH] = 0.0
    cst[:, 640:1152] = rm[None, :]
    out["cst"] = cst
    return out


def _prep_prm(inp, core):
    prm = np.zeros((128, 192), np.float32)
    g = inp["ln_g"][0].reshape(4, KC, 128)
    b = inp["ln_b"][0].reshape(4, KC, 128)
    prm[:, 0:64] = g.transpose(2, 0, 1).reshape(128, 64)
    prm[:, 64:128] = b.transpose(2, 0, 1).reshape(128, 64)
    cw = inp["conv_w"][0].reshape(3, 8, 128)
    prm[:, 128:152] = cw.transpose(2, 0, 1).reshape(128, 24)
    lbp = inp["hg_lower_bound"].reshape(2, 8, 128)
    prm[:, 152:160] = lbp[0].T
    prm[:, 160:168] = lbp[1].T
    prm[:, 168] = inp["hg_norm_w"][0]
    if core % 2 == 1:
        prm[:, 169] = 1.0
    return prm


_NC_CACHE = {}


def _get_nc(mode):
    if mode not in _NC_CACHE:
        _NC_CACHE[mode] = build(mode)
    return _NC_CACHE[mode]


F_KEYS = ("cst", "wfi1", "wfo1", "wmin", "wfi2", "wfo2", "wbr", "wmo", "wpg", "wpp")


def _to_fm(a, nchunk):
    T = a.shape[0] // NT
    return np.ascontiguousarray(a.reshape(T, NT, nchunk, 128).transpose(0, 3, 2, 1))


def kernel(**inputs):
    inp = {k: np.asarray(v) for k, v in inputs.items()}
    shared = _prep_shared(inp)
    xflat = np.ascontiguousarray(inp["x"]).reshape(NCORES * TOK, D)
    ps = np.ascontiguousarray(inp["p"][0]).reshape(NCORES, TOK, 256)
    in_maps = []
    for c in range(NCORES):
        m = {k: shared[k] for k in F_KEYS}
        lo = c * TOK
        if c % 2 == 1:
            xe = xflat[lo - NT:lo + TOK]
        else:
            xe = np.concatenate([xflat[lo:lo + NT], xflat[lo:lo + TOK]], axis=0)
        m["x"] = _to_fm(xe, KC)
        m["p"] = _to_fm(ps[c], 2)
        m["prm"] = _prep_prm(inp, c)
        in_maps.append(m)
    res = run_bass_kernel_spmd(_get_nc("F"), in_maps, core_ids=list(range(NCORES))).results
    outs = []
    for r in res:
        yf = np.asarray(r["y"])
        outs.append(yf.transpose(0, 3, 2, 1).reshape(TOK, D))
    y = np.stack(outs, axis=0)
    return np.ascontiguousarray(y.reshape(4, 4096, D)).astype(np.float32, copy=False)
```

```python
import numpy as np
from contextlib import ExitStack

import concourse.bass as bass
import concourse.mybir as mybir
from concourse.bass_utils import run_bass_kernel_spmd

F32 = mybir.dt.float32
BF16 = mybir.dt.bfloat16
AF = mybir.ActivationFunctionType
ALU = mybir.AluOpType

NCORES = 8
D = 2048
DFF = 5504
TOK = 2048
NT = 512
NTILES = TOK // NT
KC = 16
HC = 43
ALPHA = 2.0 ** 0.25
LN_EPS = 1e-5
RMS_EPS = 1e-6
CH = 64
NCH = NT // CH
RING = 3
SLOT = 8192
CCW = 1040

SELF_DEPS = ("act", "dve", "pool")


class Buf:
    __slots__ = ("ap", "names")

    def __init__(self, ap, names):
        self.ap = ap
        self.names = list(names)


def _names(bufs):
    out = []
    for b in bufs:
        if isinstance(b, Buf):
            out.extend(b.names)
        elif isinstance(b, str):
            out.append(b)
        else:
            for bb in b:
                out.extend(bb.names if isinstance(bb, Buf) else [bb])
    return out


class Prog:
    ENGS = ("pe", "act", "dve", "pool", "sp")

    def __init__(self, nc, stack, dry=False):
        self.nc = nc
        self.stack = stack
        self.dry = dry
        self.code = {e: [] for e in self.ENGS}
        self.sem = {}
        self.val = {}
        self.known = {e: {} for e in self.ENGS}
        self.bufs = {}
        self.out_ticks = []

    def getsem(self, key):
        if key not in self.sem:
            self.sem[key] = self.stack.enter_context(self.nc.semaphore(key))
            self.val[key] = 0
        return self.sem[key]

    def _need(self, eng, deps):
        best = {}
        for k, v in deps:
            if v > best.get(k, 0):
                best[k] = v
        own = "e_" + eng
        for k, v in best.items():
            if k == own and eng not in SELF_DEPS:
                continue
            if self.known[eng].get(k, 0) < v:
                self.known[eng][k] = v
                s = self.sem[k]
                self.code[eng].append(lambda e, s=s, v=v: e.wait_ge(s, v))

    def _deps(self, reads, writes):
        deps = []
        for b in reads:
            st = self.bufs.get(b)
            if st and st[0]:
                deps.append(st[0])
        for b in writes:
            st = self.bufs.get(b)
            if st:
                if st[0]:
                    deps.append(st[0])
                deps.extend(st[1])
        return deps

    def _mark(self, reads, writes, tick):
        for b in reads:
            self.bufs.setdefault(b, [None, []])[1].append(tick)
        for b in writes:
            self.bufs[b] = [tick, []]

    def op(self, eng, fn, reads=(), writes=()):
        if self.dry:
            return
        reads = _names(reads)
        writes = _names(writes)
        key = "e_" + eng
        self.getsem(key)
        self._need(eng, self._deps(reads, writes))
        self.val[key] += 1
        tick = (key, self.val[key])
        s = self.sem[key]
        self.code[eng].append(lambda e, fn=fn, s=s: fn(e).then_inc(s, 1))
        self._mark(reads, writes, tick)

    def dma(self, eng, fn, reads=(), writes=(), semkey=None, is_out=False):
        if self.dry:
            return
        reads = _names(reads)
        writes = _names(writes)
        self.getsem(semkey)
        self._need(eng, self._deps(reads, writes))
        self.val[semkey] += 16
        tick = (semkey, self.val[semkey])
        s = self.sem[semkey]
        self.code[eng].append(lambda e, fn=fn, s=s: fn(e).then_inc(s, 16))
        self._mark(reads, writes, tick)
        if is_out:
            self.out_ticks.append(tick)

    def finish(self, eng="sp"):
        if self.dry:
            return
        self._need(eng, self.out_ticks)


class WS:
    def __init__(self, P, ring, units=None):
        self.P = P
        self.ring = ring
        self.units = units
        self.seq = []
        self.ci = 0
        self.issued = 0

    def _issue(self, i):
        src, n = self.units[i]
        s = i % RING
        dst = self.ring[:, s, 0:n]
        self.P.dma("pool", lambda e, dst=dst, src=src: e.dma_start(out=dst, in_=src),
                   reads=(), writes=["ring.%d" % s], semkey="ring%d" % s)
        self.issued = i + 1

    def start(self):
        for i in range(min(RING, len(self.units))):
            self._issue(i)

    def acquire(self, src, n):
        i = self.ci
        if self.units is None:
            self.seq.append((src, n))
            return Buf(self.ring[:, i % RING, 0:n], ["ring.%d" % (i % RING)])
        assert self.units[i][1] == n
        assert i < self.issued
        return Buf(self.ring[:, i % RING, 0:n], ["ring.%d" % (i % RING)])

    def acquire_next(self, src, n):
        i = self.ci + 1
        if self.units is None:
            self.seq.append((src, n))
            return Buf(self.ring[:, i % RING, 0:n], ["ring.%d" % (i % RING)])
        assert self.units[i][1] == n and i < self.issued
        return Buf(self.ring[:, i % RING, 0:n], ["ring.%d" % (i % RING)])

    def release(self):
        i = self.ci
        self.ci += 1
        if self.units is not None and i + RING < len(self.units):
            self._issue(i + RING)


def _record(nc, P, ws, T, mode):
    xr, xb, big, S, Sb, Eall = T["xr"], T["xb"], T["big"], T["S"], T["Sb"], T["Eall"]
    mean_sb, rstd, tmpA, rb, rs = T["mean"], T["rstd"], T["tmpA"], T["rb"], T["rs"]
    kvc, sm, pT, prm, cst = T["kvc"], T["sm"], T["pT"], T["prm"], T["cst"]
    lnab, lbt, oml, identb, onesd, onesv, uprev, uh = (T["lnab"], T["lb"], T["oml"], T["identb"],
                                                       T["onesd"], T["onesv"], T["uprev"], T["uh"])
    psum = T["psum"]
    D_ = T["dram"]

    XOFF = 1 if mode == "F" else 0
    def XR(m):
        return Buf(xr[:, m, :], ["xr.%d" % m])

    def XB(m):
        return Buf(xb[:, m, :], ["xb.%d" % m])

    XB_ALL = ["xb.%d" % m for m in range(KC)]
    XR_ALL = ["xr.%d" % m for m in range(KC)]

    def cells(off, n):
        return ["big.%d" % c for c in range(off // 512, (off + n + 511) // 512)]

    def arena(off, shape, dt):
        n = int(np.prod(shape))
        nb = n * (2 if dt == F32 else 1)
        ap = big[:, off:off + nb]
        if dt == F32:
            ap = ap.bitcast(F32)
        if len(shape) == 2:
            ap = ap.rearrange("p (a b) -> p a b", b=shape[1])
        return Buf(ap, cells(off, nb))

    def sub(buf, idx, off_el_per, dt):
        raise NotImplementedError

    def bank(i):
        return Buf(psum[:, i * 512:(i + 1) * 512], ["ps%d" % i])

    def bank_bf(i):
        return Buf(psum[:, i * 512:(i + 1) * 512].bitcast(BF16), ["ps%d" % i])

    ident = Buf(cst[:, 0:128], ["cst"])
    mask = Buf(cst[0:64, 128:640], ["cst"])
    rmask = Buf(cst[:, 640:1152], ["cst"])
    IDB = Buf(identb[:], ["identb"])
    ONESD = Buf(onesd[:], ["onesd"])
    ONESV = Buf(onesv[:], ["onesv"])

    def pcol(c):
        return prm[:, c:c + 1]

    LNG0, LNB0, CW0, A00, A10, NW0, SEL0 = 0, 64, 128, 152, 160, 168, 169

    O_XIN = 0
    O_H = 0
    O_T = 0
    O_QT, O_KT, O_KHT, O_VT, O_YA, O_YB = 10240, 14336, 18432, 22528, 26624, 30720
    O_MRG = 10240

    def hbuf(j):
        return Buf(big[:, O_H + j * 512:O_H + (j + 1) * 512], ["big.%d" % (O_H // 512 + j)])

    def chunked(off, c):
        return Buf(big[:, off + c * 512:off + (c + 1) * 512], ["big.%d" % (off // 512 + c)])

    def chunked32(off, c):
        o = off + c * 1024
        return Buf(big[:, o:o + 1024].bitcast(F32), cells(o, 1024))

    def pe_multi(groups, reads, writes):
        def fn(e, groups=groups):
            ins = None
            for out_ap, pairs in groups:
                n = len(pairs)
                for i, (l, r) in enumerate(pairs):
                    ins = e.matmul(out_ap, l, r, start=(i == 0), stop=(i == n - 1))
            return ins
        P.op("pe", fn, reads, writes)

    def pe_transposes(items, reads, writes):
        def fn(e, items=items):
            ins = None
            for o, i_, idn in items:
                ins = e.transpose(o, i_, idn)
            return ins
        P.op("pe", fn, reads, writes)

    def act(out, in_, func, bias=0.0, scale=1.0, extra_reads=()):
        P.op("act", lambda e: e.activation(out.ap, in_.ap, func, bias=bias, scale=scale),
             [in_] + list(extra_reads), [out])

    def tt(out, a, b, op, eng="dve"):
        P.op(eng, lambda e: e.tensor_tensor(out.ap, a.ap, b.ap, op), [a, b], [out])

    def ts2(out, a, s1, s2, op0, op1, extra_reads=(), eng="dve"):
        P.op(eng, lambda e: e.tensor_scalar(out.ap, a.ap, s1, s2, op0, op1), [a] + list(extra_reads), [out])

    def ts1(out, a, s1, op, extra_reads=(), eng="dve"):
        P.op(eng, lambda e: e.tensor_single_scalar(out.ap, a.ap, s1, op), [a] + list(extra_reads), [out])

    def stt(out, a, s, b, op0, op1, extra_reads=(), eng="dve"):
        P.op(eng, lambda e: e.scalar_tensor_tensor(out.ap, a.ap, s, b.ap, op0, op1),
             [a, b] + list(extra_reads), [out])

    def cp(out, in_, eng="dve"):
        P.op(eng, lambda e: e.tensor_copy(out.ap, in_.ap), [in_], [out])

    def ln_prep(m):
        i = m % 2
        rbi = Buf(rb[:, i, :], ["rb.%d" % i])
        rsi = Buf(rs[:, i, :], ["rs.%d" % i])
        act(rbi, XR(m), AF.Copy)
        act(rsi, XR(m), AF.Square)

    def ln_mm(m):
        i = m % 2
        rbi = Buf(rb[:, i, :], ["rb.%d" % i])
        rsi = Buf(rs[:, i, :], ["rs.%d" % i])
        b6, b7 = bank(6), bank(7)

        def fn(e, m=m):
            e.matmul(b6.ap, onesd[:], rbi.ap, start=(m == 0), stop=(m == KC - 1), skip_group_check=True)
            return e.matmul(b7.ap, onesd[:], rsi.ap, start=(m == 0), stop=(m == KC - 1), skip_group_check=True)
        P.op("pe", fn, [rbi, rsi, ONESD], [b6, b7])

    def ln_step(m):
        if m >= 1:
            ln_mm(m - 1)

    def pipelined_groups(specs):
        ws_ = [(sl.ap.rearrange("p (k c) -> p k c", c=512), bk, cs) for sl, bk, cs in specs]
        slots = []
        for sl, _, _ in specs:
            if sl not in slots:
                slots.append(sl)
        for k in range(KC):
            def fn(e, k=k):
                ins = None
                for w, bk, cs in ws_:
                    ins = e.matmul(bk.ap, w[:, k, cs], xb[:, k, :], start=(k == 0), stop=(k == KC - 1),
                                   skip_group_check=True)
                return ins
            P.op("pe", fn, slots + ["xb.%d" % k], [bk for _, bk, _ in specs])

    def ln_apply(l, final=False, cast_next=False):
        MEAN = Buf(mean_sb[:], ["mean"])
        RSTD = Buf(rstd[:], ["rstd"])
        act(RSTD, bank(6), AF.Square)
        act(MEAN, bank(6), AF.Copy)
        stt(RSTD, bank(7), LN_EPS, RSTD, ALU.add, ALU.subtract)
        act(RSTD, RSTD, AF.Ln)
        act(RSTD, RSTD, AF.Exp, scale=-0.5)
        for m in range(KC):
            i = m % 2
            t = Buf(tmpA[:, i, :], ["tmpA.%d" % i])
            tt(t, XR(m), MEAN, ALU.subtract)
            stt(t, t, pcol(LNG0 + l * 16 + m), RSTD, ALU.mult, ALU.mult, extra_reads=["prm"])
            if not final:
                act(XB(m), t, AF.Identity, bias=pcol(LNB0 + l * 16 + m), scale=1.0, extra_reads=["prm"])
                act(XR(m), t, AF.Identity, bias=lnab[:, l * 16 + m:l * 16 + m + 1], scale=ALPHA,
                    extra_reads=["lnab"])
            else:
                act(XR(m), t, AF.Identity, bias=pcol(LNB0 + l * 16 + m), scale=1.0, extra_reads=["prm"])
                if cast_next:
                    load_x_cast(m)

    def ffn(f, l, pipelined=False, tail=None):
        wfi, wfo = D_["wfi%d" % f], D_["wfo%d" % f]

        def evac(j, pa, pu):
            sa = Buf(tmpA[:, j % 2, :], ["tmpA.%d" % (j % 2)])
            act(sa, pa, AF.Silu)
            tt(hbuf(j), sa, pu, ALU.mult)

        u0 = 0
        if pipelined:
            s0 = ws.acquire(wfi[0], 8192)
            s1 = ws.acquire_next(wfi[1], 8192)
            specs = []
            for uu, sl in ((0, s0), (1, s1)):
                for jj in range(2):
                    b0 = uu * 4 + jj * 2
                    specs.append((sl, bank(b0), slice(jj * 128, (jj + 1) * 128)))
                    specs.append((sl, bank(b0 + 1), slice(256 + jj * 128, 256 + (jj + 1) * 128)))
            pipelined_groups(specs)
            for j in range(4):
                evac(j, bank(2 * j), bank(2 * j + 1))
            ws.release()
            ws.release()
            u0 = 2
        for u in range(u0, 22):
            slot = ws.acquire(wfi[u], 8192)
            w = slot.ap.rearrange("p (k c) -> p k c", c=512)
            for jj in range(2):
                j = 2 * u + jj
                if j >= HC:
                    break
                pa = bank(0 if j % 2 == 0 else 2)
                pu = bank(1 if j % 2 == 0 else 3)
                groups = [
                    (pa.ap, [(w[:, k, jj * 128:(jj + 1) * 128], xb[:, k, :]) for k in range(KC)]),
                    (pu.ap, [(w[:, k, 256 + jj * 128:256 + (jj + 1) * 128], xb[:, k, :]) for k in range(KC)]),
                ]
                pe_multi(groups, [slot] + XB_ALL, [pa, pu])
                evac(j, pa, pu)
            ws.release()
        hall = ["big.%d" % (O_H // 512 + j) for j in range(HC)]
        for m in range(KC):
            slot = ws.acquire(wfo[m], DFF)
            w = slot.ap.rearrange("p (k c) -> p k c", c=128)
            po = bank(4 + m % 2)
            pe_multi([(po.ap, [(w[:, k, :], hbuf(k).ap) for k in range(HC)])], [slot] + hall, [po])
            ws.release()
            ln_step(m)
            stt(XR(m), po, 0.5, XR(m), ALU.mult, ALU.add)
            ln_prep(m)
        if tail is not None:
            tail()
        ln_mm(KC - 1)
        ln_apply(l)

    O_XIN2 = 16384

    def xin_bufs():
        return [Buf(big[:, O_XIN2 + b * 4096:O_XIN2 + (b + 1) * 4096].bitcast(F32), cells(O_XIN2 + b * 4096, 4096))
                for b in range(4)]

    def xin_fm():
        return [Buf(big[:, O_XIN2 + g * 4096:O_XIN2 + (g + 1) * 4096].bitcast(F32).rearrange("p (m j) -> p m j", j=NT),
                    cells(O_XIN2 + g * 4096, 4096)) for g in range(4)]

    def load_x_dma(t):
        x_d = D_["x"]
        t = t + XOFF
        xin = xin_fm()
        for g in range(4):
            src = x_d[t][:, g * 4:(g + 1) * 4, :]
            P.dma("sp", lambda e, o=xin[g].ap, s=src: e.dma_start(out=o, in_=s), [], [xin[g]], semkey="xin%d" % g)

    def xin_chunk(m):
        xin = xin_fm()
        return Buf(xin[m // 4].ap[:, m % 4, :], xin[m // 4].names)

    def load_x_cast(m):
        act(XB(m), xin_chunk(m), AF.Copy)

    def load_x_scale():
        for m in range(KC):
            ts1(XR(m), xin_chunk(m), ALPHA, ALU.mult)

    def load_x_tr():
        for m in range(KC):
            load_x_cast(m)
        load_x_scale()

    def load_x(t):
        load_x_dma(t)
        load_x_tr()

    def store_y(t):
        y_d = D_["y"]
        for g in range(4):
            P.dma("sp", lambda e, g=g: e.dma_start(out=y_d[t][:, g * 4:(g + 1) * 4, :], in_=xr[:, g * 4:(g + 1) * 4, :]),
                  ["xr.%d" % (g * 4 + mm) for mm in range(4)], [], semkey="yout%d" % g, is_out=True)

    TMP = [chunked32(O_T + 4096, i) for i in range(6)]
    EC = [chunked32(O_T, i) for i in range(4)]

    def QT(c):
        return chunked(O_QT, c)

    def KT(c):
        return chunked(O_KT, c)

    def KHT(c):
        return chunked(O_KHT, c)

    def VT(c):
        return chunked(O_VT, c)

    def zgroup(slot, jj, bk):
        w = slot.ap.rearrange("p (k c) -> p k c", c=512)
        pe_multi([(bk.ap, [(w[:, k, jj * 128:(jj + 1) * 128], xb[:, k, :]) for k in range(KC)])],
                 [slot] + XB_ALL, [bk])

    def f_head(c, slot, jj, full):
        bk = bank(c % 4)
        zgroup(slot, jj, bk)
        T1, T2, T3, T4, T5, T6 = TMP
        ecc = EC[c % 4] if full else T4
        act(T1, bk, AF.Sigmoid)
        ts2(T1, T1, oml[:, c:c + 1], lbt[:, c:c + 1], ALU.mult, ALU.add, extra_reads=["lb"])
        act(T2, T1, AF.Ln)
        P.op("dve", lambda e: e.tensor_tensor_scan(T3.ap, rmask.ap, T2.ap, 0.0, ALU.mult, ALU.add),
             [rmask, T2], [T3])
        act(ecc, T3, AF.Exp)
        act(T5, T3, AF.Exp, scale=-1.0)
        ts2(T1, T1, -1.0, 1.0, ALU.mult, ALU.add)
        tt(T2, T1, T5, ALU.mult)
        if full:
            act(KT(c), T2, AF.Copy)
        ec3 = ecc.ap.rearrange("p (c j) -> p c j", j=CH)
        ends = ec3[:, :, CH - 1:CH]
        cp(Buf(Eall[:, c, :], ["Eall"]), Buf(ec3[:, :, CH - 1], ecc.names))
        k3 = Buf(T2.ap.rearrange("p (c j) -> p c j", j=CH), T2.names)
        o3 = Buf(KHT(c).ap.rearrange("p (c j) -> p c j", j=CH), KHT(c).names)
        tt(o3, k3, Buf(ends.to_broadcast([128, NCH, CH]), ecc.names), ALU.mult)

    def f_unit(hh, slot, full):
        A = TMP[0:4]
        Bf = EC
        X, Y = TMP[4], TMP[5]
        for jj in range(4):
            c = hh * 4 + jj
            bk = bank(jj)
            zgroup(slot, jj, bk)
            act(A[jj], bk, AF.Sigmoid)
            ts2(A[jj], A[jj], oml[:, c:c + 1], lbt[:, c:c + 1], ALU.mult, ALU.add, extra_reads=["lb"])
        for jj in range(4):
            act(Bf[jj], A[jj], AF.Ln)
            ts2(A[jj], A[jj], -1.0, 1.0, ALU.mult, ALU.add)
        XS = [TMP[4], Buf(tmpA[:, 0, :], ["tmpA.0"])]
        YS = [TMP[5], Buf(tmpA[:, 1, :], ["tmpA.1"])]
        for jj in range(4):
            c = hh * 4 + jj
            X, Y = XS[jj % 2], YS[jj % 2]
            P.op("dve", lambda e, jj=jj, X=X: e.tensor_tensor_scan(X.ap, rmask.ap, Bf[jj].ap, 0.0, ALU.mult, ALU.add),
                 [rmask, Bf[jj]], [X])
            act(Bf[jj], X, AF.Exp)
            act(Y, X, AF.Exp, scale=-1.0)
            tt(A[jj], A[jj], Y, ALU.mult)
            if full:
                act(KT(c), A[jj], AF.Copy)
            ec3 = Bf[jj].ap.rearrange("p (c j) -> p c j", j=CH)
            ends = ec3[:, :, CH - 1:CH]
            cp(Buf(Eall[:, c, :], ["Eall"]), Buf(ec3[:, :, CH - 1], Bf[jj].names))
            k3 = Buf(A[jj].ap.rearrange("p (c j) -> p c j", j=CH), A[jj].names)
            o3 = Buf(KHT(c).ap.rearrange("p (c j) -> p c j", j=CH), KHT(c).names)
            tt(o3, k3, Buf(ends.to_broadcast([128, NCH, CH]), Bf[jj].names), ALU.mult)

    def q_head(c, slot, jj):
        bk = bank(4 + c % 4)
        zgroup(slot, jj, bk)
        T6 = TMP[5]
        act(T6, bk, AF.Silu)
        tt(QT(c), T6, EC[c % 4], ALU.mult)

    def v_head(c, slot, jj, k):
        bk = bank(c % 4)
        zgroup(slot, jj, bk)
        if k % 2 == 0:
            act(VT(c), bk, AF.Copy)
        else:
            cp(VT(c), bk)

    O_O = O_T

    def scan(full):
        KHT_ALL = cells(O_KHT, 4096)
        VT_ALL = cells(O_VT, 4096)
        QT_ALL = cells(O_QT, 4096)
        KT_ALL = cells(O_KT, 4096)
        O_ALL = cells(O_O, 8192)
        o3 = big[:, O_O:O_O + 8192].bitcast(F32).rearrange("p (c t) -> p c t", t=NT)
        SB_ = Buf(S[:], ["S"])
        def transposes(cc):
            i = cc % 2
            ptk, ptv = bank_bf(0), bank_bf(1)
            items = [(ptk.ap[0:CH, c * 128:(c + 1) * 128], big[:, O_KHT + c * 512 + cc * CH:O_KHT + c * 512 + (cc + 1) * CH],
                      identb[:]) for c in range(8)]
            items += [(ptv.ap[0:CH, c * 128:(c + 1) * 128], big[:, O_VT + c * 512 + cc * CH:O_VT + c * 512 + (cc + 1) * CH],
                       identb[:]) for c in range(8)]
            pe_transposes(items, KHT_ALL + VT_ALL + [IDB], [ptk, ptv])
            khc = Buf(kvc[0:CH, i, :], ["kvc.%d" % i])
            vc = Buf(kvc[0:CH, 2 + i, :], ["kvc.%d" % (2 + i)])
            act(khc, Buf(ptk.ap[0:CH, :], ptk.names), AF.Copy)
            cp(vc, Buf(ptv.ap[0:CH, :], ptv.names))

        transposes(0)
        for cc in range(NCH):
            sl = slice(cc * CH, (cc + 1) * CH)
            i = cc % 2
            khc = Buf(kvc[0:CH, i, :], ["kvc.%d" % i])
            vc = Buf(kvc[0:CH, 2 + i, :], ["kvc.%d" % (2 + i)])
            pU = Buf(psum[:, 4 * 512:6 * 512], ["ps4", "ps5"])
            groups = [(pU.ap[:, c * 128:(c + 1) * 128],
                       [(khc.ap[:, c * 128:(c + 1) * 128], vc.ap[:, c * 128:(c + 1) * 128])]) for c in range(8)]
            pe_multi(groups, [khc, vc], [pU])
            if full:
                pS = bank(2)
                groups = [(pS.ap[0:CH, c * CH:(c + 1) * CH],
                           [(big[:, O_KT + c * 512 + cc * CH:O_KT + c * 512 + (cc + 1) * CH],
                             big[:, O_QT + c * 512 + cc * CH:O_QT + c * 512 + (cc + 1) * CH])]) for c in range(8)]
                pe_multi(groups, KT_ALL + QT_ALL, [pS])
                smi = Buf(sm[0:CH, i, :], ["sm.%d" % i])
                tt(smi, Buf(pS.ap[0:CH, :], pS.names), mask, ALU.mult)
            if cc + 1 < NCH:
                transposes(cc + 1)
            if full:
                pO = bank(3)
                cur = cc % 2
                groups = []
                for c in range(8):
                    groups.append((pO.ap[:, c * CH:(c + 1) * CH],
                                   [(Sb[:, cur, c * 128:(c + 1) * 128],
                                     big[:, O_QT + c * 512 + cc * CH:O_QT + c * 512 + (cc + 1) * CH]),
                                    (vc.ap[:, c * 128:(c + 1) * 128], smi.ap[:, c * CH:(c + 1) * CH])]))
                pe_multi(groups, ["Sb.%d" % cur, vc, smi] + QT_ALL, [pO])
                act(Buf(o3[:, :, sl], O_ALL), Buf(pO.ap.rearrange("p (c t) -> p c t", t=CH), pO.names), AF.Copy)
            S3 = Buf(S[:].rearrange("p (c v) -> p c v", v=128), ["S"])
            eb = Buf(Eall[:, :, cc:cc + 1].to_broadcast([128, 8, 128]), ["Eall"])
            tt(S3, S3, eb, ALU.mult)
            tt(SB_, SB_, pU, ALU.add)
            if full:
                nxt = (cc + 1) % 2
                act(Buf(Sb[:, nxt, :], ["Sb.%d" % nxt]), SB_, AF.Copy)

    def mixer_a(halo):
        wmin = D_["wmin"]
        for hh in range(2):
            slot = ws.acquire(wmin[8 + hh], 8192)
            f_unit(hh, slot, full=False)
            ws.release()
        k = 0
        for hh in range(2):
            slot = ws.acquire(wmin[10 + hh], 8192)
            for jj in range(4):
                v_head(hh * 4 + jj, slot, jj, k)
                k += 1
            ws.release()
        scan(full=False)
        if halo:
            cs = chunked32(O_T + 4096, 0)
            c3 = Buf(cs.ap[:, 0:16].rearrange("p (c t) -> p c t", t=2), cs.names)
            for which in range(2):
                for hh in range(2):
                    slot = ws.acquire(wmin[2 + 2 * which + hh], 8192)
                    w = slot.ap.rearrange("p (k c) -> p k c", c=512)
                    bk = bank(hh)
                    groups = [(bk.ap[:, jj * 2:jj * 2 + 2],
                               [(w[:, k, jj * 128:(jj + 1) * 128], xb[:, k, NT - 2:NT]) for k in range(KC)])
                              for jj in range(4)]
                    pe_multi(groups, [slot] + XB_ALL, [bk])
                    ws.release()
                    src = Buf(bk.ap[:, 0:8].rearrange("p (c t) -> p c t", t=2), bk.names)
                    dst_c = Buf(c3.ap[:, hh * 4:(hh + 1) * 4, :], c3.names)
                    dst_u = Buf(uh[:, hh * 8:(hh + 1) * 8].rearrange("p (c t) -> p c t", t=2), ["uh"])
                    if which == 0:
                        cp(dst_c, src)
                    else:
                        tt(dst_u, src, dst_c, ALU.mult)

    def mixer_b(t):
        wmin, wbr, wmo = D_["wmin"], D_["wbr"], D_["wmo"]
        for hf in range(2):
            Cs = [chunked32(O_T, q) for q in range(4)]
            ubuf = [Buf(big[:, O_T + 4096 + q * 1032:O_T + 4096 + (q + 1) * 1032].bitcast(F32),
                        cells(O_T + 4096 + q * 1032, 1032)) for q in range(4)]
            slot = ws.acquire(wmin[2 + hf], 8192)
            if hf == 0:
                slot_h = ws.acquire_next(wmin[4 + hf], 8192)
                pipelined_groups([(slot, bank(q), slice(q * 128, (q + 1) * 128)) for q in range(4)] +
                                 [(slot_h, bank(4 + q), slice(q * 128, (q + 1) * 128)) for q in range(4)])
            for q in range(4):
                bk = bank(q)
                if hf != 0:
                    zgroup(slot, q, bk)
                act(Cs[q], bk, AF.Copy)
            ws.release()
            if hf != 0:
                slot_h = ws.acquire(wmin[4 + hf], 8192)
            for q in range(4):
                c = hf * 4 + q
                bk = bank(4 + q) if hf == 0 else bank(q)
                if hf != 0:
                    zgroup(slot_h, q, bk)
                ub = ubuf[q]
                tt(Buf(ub.ap[:, 2:2 + NT], ub.names), bk, Cs[q], ALU.mult)
                act(Buf(ub.ap[:, 0:2], ub.names), Buf(uprev[:, c * 2:c * 2 + 2], ["uprev"]), AF.Copy)
                ts1(Cs[q], Buf(ub.ap[:, 0:NT], ub.names), pcol(CW0 + 0 * 8 + c), ALU.mult, extra_reads=["prm"])
                stt(Cs[q], Buf(ub.ap[:, 1:1 + NT], ub.names), pcol(CW0 + 1 * 8 + c), Cs[q], ALU.mult, ALU.add,
                    extra_reads=["prm"])
                stt(Cs[q], Buf(ub.ap[:, 2:2 + NT], ub.names), pcol(CW0 + 2 * 8 + c), Cs[q], ALU.mult, ALU.add,
                    extra_reads=["prm"])
                act(Buf(uprev[:, c * 2:c * 2 + 2], ["uprev"]), Buf(ub.ap[:, NT:NT + 2], ub.names), AF.Copy)
            ws.release()
            slot = ws.acquire(wmin[0 + hf], 8192)
            for q in range(4):
                c = hf * 4 + q
                bk = bank(q)
                zgroup(slot, q, bk)
                tt(chunked(O_YA, c), bk, Cs[q], ALU.mult)
            ws.release()
        for hh in range(2):
            slot = ws.acquire(wmin[8 + hh], 8192)
            f_unit(hh, slot, full=True)
            ws.release()
            slot = ws.acquire(wmin[6 + hh], 8192)
            for jj in range(4):
                q_head(hh * 4 + jj, slot, jj)
            ws.release()
        k = 0
        for hh in range(2):
            slot = ws.acquire(wmin[10 + hh], 8192)
            for jj in range(4):
                v_head(hh * 4 + jj, slot, jj, k)
                k += 1
            ws.release()
        scan(full=True)
        ple_prep(t)
        for hh in range(2):
            slot = ws.acquire(wmin[12 + hh], 8192)
            for jj in range(4):
                c = hh * 4 + jj
                oc = chunked32(O_O, c)
                i = c % 2
                osq = Buf(rb[:, i, :], ["rb.%d" % i])
                act(osq, oc, AF.Square)
                bm = bank(6 + i)
                pe_multi([(bm.ap, [(onesv[:], osq.ap)])], [osq, ONESV], [bm])
                rsq = chunked32(O_O + 8192, 0)
                ts1(rsq, bm, RMS_EPS, ALU.add)
                act(rsq, rsq, AF.Ln)
                act(rsq, rsq, AF.Exp, scale=-0.5)
                stt(oc, oc, pcol(NW0), rsq, ALU.mult, ALU.mult, extra_reads=["prm"])
                bk = bank(jj)
                zgroup(slot, jj, bk)
                sg = chunked32(O_O + 9216, 0)
                act(sg, bk, AF.Silu)
                tt(chunked(O_YB, c), oc, sg, ALU.mult)
            ws.release()
        YA_ALL = cells(O_YA, 4096)
        YB_ALL = cells(O_YB, 4096)
        for mg in range(4):
            sgt = [chunked32(O_T, q) for q in range(4)]
            mgt = [chunked32(O_T + 4096, q) for q in range(4)]
            for br in range(2):
                slot = ws.acquire(wmin[14 + 4 * br + mg], 8192)
                for q in range(4):
                    bk = bank(q)
                    zgroup(slot, q, bk)
                    act(sgt[q], bk, AF.Sigmoid)
                ws.release()
                slot = ws.acquire(wbr[mg * 2 + br], 4096)
                w = slot.ap.rearrange("p (k c) -> p k c", c=512)
                yoff = O_YA if br == 0 else O_YB
                for q in range(4):
                    m = mg * 4 + q
                    bk = bank(4 + q % 2)
                    pe_multi([(bk.ap, [(w[:, k, q * 128:(q + 1) * 128], big[:, yoff + k * 512:yoff + (k + 1) * 512])
                                       for k in range(8)])], [slot] + (YA_ALL if br == 0 else YB_ALL), [bk])
                    if br == 0:
                        tt(mgt[q], sgt[q], bk, ALU.mult)
                    else:
                        tt(sgt[q], sgt[q], bk, ALU.mult)
                        tt(chunked(O_MRG, m), sgt[q], mgt[q], ALU.add)
                ws.release()
        MRG_ALL = cells(O_MRG, 8192)
        for mq in range(4):
            slot = ws.acquire(wmo[mq], 8192)
            w = slot.ap.rearrange("p (k c) -> p k c", c=512)
            for q in range(4):
                m = mq * 4 + q
                bk = bank(4 + m % 2)
                pe_multi([(bk.ap, [(w[:, k, q * 128:(q + 1) * 128], big[:, O_MRG + k * 512:O_MRG + (k + 1) * 512])
                                   for k in range(KC)])], [slot] + MRG_ALL, [bk])
                ln_step(m)
                tt(XR(m), XR(m), bk, ALU.add)
                ln_prep(m)
            ws.release()
        ln_mm(KC - 1)
        ln_apply(1)

    O_PP = 16384

    def ple_prep(t):
        p_d = D_["p"]
        pin = Buf(kvc[:, 0:2, :].bitcast(F32), ["kvc.0", "kvc.1"])
        P.dma("sp", lambda e: e.dma_start(out=pin.ap, in_=p_d[t]), [], [pin], semkey="pin")
        for kc in range(2):
            act(Buf(pT[:, kc, :], ["pT"]), Buf(pin.ap[:, kc, :], pin.names), AF.Copy)

    def ple_pp():
        wpp = D_["wpp"]
        pslot = ws.acquire(wpp[0], 4096)
        wp = pslot.ap.rearrange("p (k c) -> p k c", c=2048)
        for m in range(KC):
            bb = bank(m % 4)
            pe_multi([(bb.ap, [(wp[:, k, m * 128:(m + 1) * 128], pT[:, k, :]) for k in range(2)])],
                     [pslot, "pT"], [bb])
            if m % 2 == 0:
                act(chunked32(O_PP, m), bb, AF.Copy)
            else:
                cp(chunked32(O_PP, m), bb)
        ws.release()

    def ple(t):
        wpg = D_["wpg"]
        for mq in range(4):
            slot = ws.acquire(wpg[mq], 8192)
            w = slot.ap.rearrange("p (k c) -> p k c", c=512)
            for q in range(4):
                m = mq * 4 + q
                ba = bank(m % 4)
                if mq == 0:
                    if q == 0:
                        pipelined_groups([(slot, bank(qq), slice(qq * 128, (qq + 1) * 128)) for qq in range(4)])
                else:
                    pe_multi([(ba.ap, [(w[:, k, q * 128:(q + 1) * 128], xb[:, k, :]) for k in range(KC)])],
                             [slot] + XB_ALL, [ba])
                ln_step(m)
                tg = Buf(tmpA[:, m % 2, :], ["tmpA.%d" % (m % 2)])
                act(tg, ba, AF.Sigmoid)
                tt(tg, tg, chunked32(O_PP, m), ALU.mult)
                tt(XR(m), XR(m), tg, ALU.add)
                ln_prep(m)
            ws.release()
        ln_mm(KC - 1)
        if t + 1 < NTILES:
            load_x_dma(t + 1)
        ln_apply(3, final=True, cast_next=(t + 1 < NTILES))

    P.dma("sp", lambda e: e.dma_start(out=prm[:], in_=D_["prm"]), [], ["prm"], semkey="prm")
    P.dma("sp", lambda e: e.dma_start(out=cst[:], in_=D_["cst"]), [], ["cst"], semkey="cst")
    P.op("dve", lambda e: e.tensor_single_scalar(lnab[:], prm[:, LNB0:LNB0 + 64], ALPHA, ALU.mult), ["prm"], ["lnab"])
    P.op("dve", lambda e: e.tensor_tensor(lbt[:], prm[:, A00:A00 + 8], prm[:, A10:A10 + 8], ALU.subtract), ["prm"], ["lb"])
    P.op("act", lambda e: e.activation(lbt[:], lbt[:], AF.Sigmoid), ["lb"], ["lb"])
    P.op("dve", lambda e: e.tensor_scalar(oml[:], lbt[:], -1.0, 1.0, ALU.mult, ALU.add), ["lb"], ["lb"])
    P.op("act", lambda e: e.activation(identb[:], cst[:, 0:128], AF.Copy), ["cst"], ["identb"])
    P.op("dve", lambda e: e.memset(onesd[:], 1.0 / D), [], ["onesd"])
    P.op("dve", lambda e: e.memset(onesv[:], 1.0 / 128.0), [], ["onesv"])
    P.op("dve", lambda e: e.memset(S[:], 0.0), [], ["S"])
    P.op("dve", lambda e: e.memset(uh[:], 0.0), [], ["uh"])
    if ws.units is not None:
        ws.start()

    if mode == "F":
        load_x(-1)
        ffn(1, 0, pipelined=True)
        mixer_a(True)
        SBF = Buf(S[:], ["S"])
        ts1(SBF, SBF, pcol(SEL0), ALU.mult, extra_reads=["prm"])
        ts1(Buf(uprev[:], ["uprev"]), Buf(uh[:], ["uh"]), pcol(SEL0), ALU.mult, extra_reads=["prm"])
        P.op("act", lambda e: e.activation(Sb[:, 0, :], S[:], AF.Copy), ["S"], ["Sb.0"])
        load_x_dma(0)
        for t in range(NTILES):
            if t == 0:
                load_x_tr()
            else:
                load_x_scale()
            ffn(1, 0, pipelined=True)
            mixer_b(t)
            ffn(2, 2, pipelined=True, tail=ple_pp)
            ple(t)
            store_y(t)
        P.finish("sp")
        return

    if mode in ("A", "AB"):
        for t in range(NTILES):
            load_x(t)
            ffn(1, 0)
            P.dma("sp", lambda e, t=t: e.dma_start(out=D_["xs"][t], in_=xr[:].rearrange("p a b -> p (a b)")),
                  XR_ALL, ["xs.%d" % t], semkey="spill", is_out=(mode == "A"))
            mixer_a(t == NTILES - 1)

    SB_ = Buf(S[:], ["S"])
    UP = Buf(uprev[:], ["uprev"])
    if mode == "A":
        ccin = D_["ccin"]
        P.dma("sp", lambda e: e.dma_start(out=ccin[:, 0:1024], in_=S[:]), ["S"], ["ccin"], semkey="ccs", is_out=True)
        P.dma("sp", lambda e: e.dma_start(out=ccin[:, 1024:CCW], in_=uh[:]), ["uh"], ["ccin"], semkey="ccs", is_out=True)
    elif mode == "B":
        gin = D_["gin"]
        P.dma("sp", lambda e: e.dma_start(out=S[:], in_=gin[:, 0:1024]), [], ["S"], semkey="gin")
        P.dma("sp", lambda e: e.dma_start(out=uprev[:], in_=gin[:, 1024:CCW]), [], ["uprev"], semkey="gin")
    else:
        ccin, ccout = D_["ccin"], D_["ccout"]
        P.dma("sp", lambda e: e.dma_start(out=ccin[:, 0:1024], in_=S[:]), ["S"], ["ccin"], semkey="ccs")
        P.dma("sp", lambda e: e.dma_start(out=ccin[:, 1024:CCW], in_=uh[:]), ["uh"], ["ccin"], semkey="ccs")
        P.dma("pool", lambda e: e.collective_compute("AllGather", ALU.bypass, replica_groups=[list(range(NCORES))],
                                                     ins=[ccin], outs=[ccout]),
              ["ccin"], ["ccout"], semkey="cc")
        G = arena(0, [NCORES, CCW], F32)
        P.dma("sp", lambda e: e.dma_start(out=G.ap, in_=ccout.rearrange("(r p) f -> p r f", p=128)),
              ["ccout"], [G], semkey="ccg")
        for r in range(NCORES):
            gs = Buf(G.ap[:, r, 0:1024], G.names)
            gu = Buf(G.ap[:, r, 1024:CCW], G.names)
            if r == 0:
                ts1(SB_, gs, pcol(SEL0 + r), ALU.mult, extra_reads=["prm"])
                ts1(UP, gu, pcol(SEL0 + r), ALU.mult, extra_reads=["prm"])
            else:
                stt(SB_, gs, pcol(SEL0 + r), SB_, ALU.mult, ALU.add, extra_reads=["prm"])
                stt(UP, gu, pcol(SEL0 + r), UP, ALU.mult, ALU.add, extra_reads=["prm"])

    if mode in ("B", "AB"):
        P.op("act", lambda e: e.activation(Sb[:, 0, :], S[:], AF.Copy), ["S"], ["Sb.0"])
        for t in range(NTILES):
            P.dma("sp", lambda e, t=t: e.dma_start(out=xr[:].rearrange("p a b -> p (a b)"), in_=D_["xs"][t]),
                  ["xs.%d" % t], XR_ALL, semkey="reload")
            for m in range(KC):
                act(XB(m), XR(m), AF.Copy, scale=1.0 / ALPHA)
            mixer_b(t)
            ffn(2, 2)
            ple(t)
            store_y(t)
    P.finish("sp")


def _ple_fix_note():
    pass


def build(mode):
    nc = bass.Bass("TRN2", target_bir_lowering=False)
    dram = {}

    def din(name, shape):
        dram[name] = nc.dram_tensor(name, shape, F32, kind="ExternalInput").ap()

    if mode in ("A", "AB"):
        din("x", [TOK, D])
    if mode == "F":
        din("x", [NTILES + 1, 128, KC, NT])
    din("prm", [128, 192])
    din("cst", [128, 1152])
    if mode in ("A", "AB", "F"):
        din("wfi1", [22, 128, 8192])
        din("wfo1", [16, 128, DFF])
    din("wmin", [22, 128, 8192])
    if mode in ("B", "AB", "F"):
        din("p", [NTILES, 128, 2, NT])
        din("wfi2", [22, 128, 8192])
        din("wfo2", [16, 128, DFF])
        din("wbr", [8, 128, 4096])
        din("wmo", [4, 128, 8192])
        din("wpg", [4, 128, 8192])
        din("wpp", [1, 128, 4096])
        dram["y"] = nc.dram_tensor("y", [NTILES, 128, KC, NT], F32, kind="ExternalOutput").ap()
    if mode == "A":
        dram["xs"] = nc.dram_tensor("xs", [NTILES, 128, 8192], F32, kind="ExternalOutput").ap()
        dram["ccin"] = nc.dram_tensor("ccin", [128, CCW], F32, kind="ExternalOutput").ap()
    elif mode == "B":
        din("xs", [NTILES, 128, 8192])
        din("gin", [128, CCW])
    elif mode == "AB":
        dram["xs"] = nc.dram_tensor("xs", [NTILES, 128, 8192], F32).ap()
        dram["ccin"] = nc.dram_tensor("ccin", [128, CCW], F32).ap()
        dram["ccout"] = nc.dram_tensor("ccout", [NCORES * 128, CCW], F32).ap()

    with ExitStack() as st:
        def sb(name, shape, dt):
            return st.enter_context(nc.sbuf_tensor("s_" + name, shape, dt))

        T = {"dram": dram}
        T["xr"] = sb("xr", [128, KC, NT], F32)
        T["xb"] = sb("xb", [128, KC, NT], BF16)
        T["big"] = sb("big", [128, 34816], BF16)
        ring = sb("ring", [128, RING, SLOT], BF16)
        T["S"] = sb("S", [128, 1024], F32)
        T["Sb"] = sb("Sb", [128, 2, 1024], BF16)
        T["Eall"] = sb("Eall", [128, 8, NCH], F32)
        T["mean"] = sb("mean", [128, NT], F32)
        T["rstd"] = sb("rstd", [128, NT], F32)
        T["tmpA"] = sb("tmpA", [128, 2, NT], F32)
        T["rb"] = sb("rb", [128, 2, NT], BF16)
        T["rs"] = sb("rs", [128, 2, NT], BF16)
        T["kvc"] = sb("kvc", [128, 4, 1024], BF16)
        T["sm"] = sb("sm", [128, 2, NT], BF16)
        T["pT"] = sb("pT", [128, 2, NT], BF16)
        T["prm"] = sb("prm", [128, 192], F32)
        T["cst"] = sb("cst", [128, 1152], F32)
        T["lnab"] = sb("lnab", [128, 64], F32)
        T["lb"] = sb("lb", [128, 8], F32)
        T["oml"] = sb("oml", [128, 8], F32)
        T["identb"] = sb("identb", [128, 128], BF16)
        T["onesd"] = sb("onesd", [128, 128], BF16)
        T["onesv"] = sb("onesv", [128, 128], BF16)
        T["uprev"] = sb("uprev", [128, 16], F32)
        T["uh"] = sb("uh", [128, 16], F32)
        T["psum"] = st.enter_context(nc.psum_tensor("psum", [128, 4096], F32))

        Pd = Prog(nc, st, dry=True)
        wsd = WS(Pd, ring, None)
        _record(nc, Pd, wsd, T, mode)
        units = wsd.seq
        P = Prog(nc, st, dry=False)
        ws = WS(P, ring, units)
        _record(nc, P, ws, T, mode)
        assert ws.ci == len(units), (ws.ci, len(units))

        with nc.Block() as block:
            @block.tensor
            def _(e):
                for f in P.code["pe"]:
                    f(e)

            @block.scalar
            def _(e):
                for f in P.code["act"]:
                    f(e)

            @block.vector
            def _(e):
                for f in P.code["dve"]:
                    f(e)

            @block.gpsimd
            def _(e):
                for f in P.code["pool"]:
                    f(e)

            @block.sync
            def _(e):
                for f in P.code["sp"]:
                    f(e)
    return nc


def _k_units(W, ncol):
    K, C = W.shape
    kc = K // 128
    u = C // ncol
    a = W.reshape(kc, 128, u, ncol).transpose(2, 1, 0, 3)
    return np.ascontiguousarray(a).reshape(u, 128, kc * ncol)


def _prep_shared(inp):
    out = {}
    for f, (wi, wo) in ((1, ("ffn1_w_in", "ffn1_w_out")), (2, ("ffn2_w_in", "ffn2_w_out"))):
        W = inp[wi][0]
        Wp = np.zeros((D, 2, 22 * 256), np.float32)
        Wp[:, 0, :DFF] = W[:, :DFF]
        Wp[:, 1, :DFF] = W[:, DFF:]
        a = Wp.reshape(KC, 128, 2, 22, 256).transpose(3, 1, 0, 2, 4)
        out["wfi%d" % f] = np.ascontiguousarray(a).reshape(22, 128, 8192)
        out["wfo%d" % f] = _k_units(inp[wo][0], 128)
    out["wmin"] = _k_units(inp["mix_w_in"][0], 512)
    wa = _k_units(inp["branch_w_conv"][0], 512)
    wb = _k_units(inp["branch_w_hgrn"][0], 512)
    out["wbr"] = np.ascontiguousarray(np.stack([wa, wb], axis=1)).reshape(8, 128, 4096)
    out["wmo"] = _k_units(inp["mix_w_out"][0], 512)
    out["wpg"] = _k_units(inp["ple_w_gate"][0], 512)
    out["wpp"] = _k_units(inp["ple_w_proj"][0], 2048)
    cst = np.zeros((128, 1152), np.float32)
    cst[:, 0:128] = np.eye(128, dtype=np.float32)
    s = np.arange(CH)[:, None]
    tq = np.arange(CH)[None, :]
    m = (s <= tq).astype(np.float32)
    cst[0:CH, 128:640] = np.tile(m, (1, 8))
    rm = np.ones(NT, np.float32)
    rm[::CH] = 0.0
    cst[:, 640:1152] = rm[None, :]
    out["cst"] = cst
    return out


def _prep_prm(inp, core):
    prm = np.zeros((128, 192), np.float32)
    g = inp["ln_g"][0].reshape(4, KC, 128)
    b = inp["ln_b"][0].reshape(4, KC, 128)
    prm[:, 0:64] = g.transpose(2, 0, 1).reshape(128, 64)
    prm[:, 64:128] = b.transpose(2, 0, 1).reshape(128, 64)
    cw = inp["conv_w"][0].reshape(3, 8, 128)
    prm[:, 128:152] = cw.transpose(2, 0, 1).reshape(128, 24)
    lbp = inp["hg_lower_bound"].reshape(2, 8, 128)
    prm[:, 152:160] = lbp[0].T
    prm[:, 160:168] = lbp[1].T
    prm[:, 168] = inp["hg_norm_w"][0]
    if core % 2 == 1:
        prm[:, 169] = 1.0
    return prm


_NC_CACHE = {}


def _get_nc(mode):
    if mode not in _NC_CACHE:
        _NC_CACHE[mode] = build(mode)
    return _NC_CACHE[mode]


F_KEYS = ("cst", "wfi1", "wfo1", "wmin", "wfi2", "wfo2", "wbr", "wmo", "wpg", "wpp")


def _to_fm(a, nchunk):
    T = a.shape[0] // NT
    return np.ascontiguousarray(a.reshape(T, NT, nchunk, 128).transpose(0, 3, 2, 1))


def kernel(**inputs):
    inp = {k: np.asarray(v) for k, v in inputs.items()}
    shared = _prep_shared(inp)
    xflat = np.ascontiguousarray(inp["x"]).reshape(NCORES * TOK, D)
    ps = np.ascontiguousarray(inp["p"][0]).reshape(NCORES, TOK, 256)
    in_maps = []
    for c in range(NCORES):
        m = {k: shared[k] for k in F_KEYS}
        lo = c * TOK
        if c % 2 == 1:
            xe = xflat[lo - NT:lo + TOK]
        else:
            xe = np.concatenate([xflat[lo:lo + NT], xflat[lo:lo + TOK]], axis=0)
        m["x"] = _to_fm(xe, KC)
        m["p"] = _to_fm(ps[c], 2)
        m["prm"] = _prep_prm(inp, c)
        in_maps.append(m)
    res = run_bass_kernel_spmd(_get_nc("F"), in_maps, core_ids=list(range(NCORES))).results
    outs = []
    for r in res:
        yf = np.asarray(r["y"])
        outs.append(yf.transpose(0, 3, 2, 1).reshape(TOK, D))
    y = np.stack(outs, axis=0)
    return np.ascontiguousarray(y.reshape(4, 4096, D)).astype(np.float32, copy=False)
```

```python
import numpy as np
from contextlib import ExitStack

import concourse.bass as bass
import concourse.mybir as mybir
from concourse.bass_utils import run_bass_kernel_spmd

F32 = mybir.dt.float32
BF16 = mybir.dt.bfloat16
AF = mybir.ActivationFunctionType
ALU = mybir.AluOpType

NCORES = 8
D = 2048
DFF = 5504
TOK = 2048
NT = 512
NTILES = TOK // NT
KC = 16
HC = 43
ALPHA = 2.0 ** 0.25
LN_EPS = 1e-5
RMS_EPS = 1e-6
CH = 64
NCH = NT // CH
RING = 3
SLOT = 8192
CCW = 1040

SELF_DEPS = ("act", "dve", "pool")


class Buf:
    __slots__ = ("ap", "names")

    def __init__(self, ap, names):
        self.ap = ap
        self.names = list(names)


def _names(bufs):
    out = []
    for b in bufs:
        if isinstance(b, Buf):
            out.extend(b.names)
        elif isinstance(b, str):
            out.append(b)
        else:
            for bb in b:
                out.extend(bb.names if isinstance(bb, Buf) else [bb])
    return out


class Prog:
    ENGS = ("pe", "act", "dve", "pool", "sp")

    def __init__(self, nc, stack, dry=False):
        self.nc = nc
        self.stack = stack
        self.dry = dry
        self.code = {e: [] for e in self.ENGS}
        self.sem = {}
        self.val = {}
        self.known = {e: {} for e in self.ENGS}
        self.bufs = {}
        self.out_ticks = []

    def getsem(self, key):
        if key not in self.sem:
            self.sem[key] = self.stack.enter_context(self.nc.semaphore(key))
            self.val[key] = 0
        return self.sem[key]

    def _need(self, eng, deps):
        best = {}
        for k, v in deps:
            if v > best.get(k, 0):
                best[k] = v
        own = "e_" + eng
        for k, v in best.items():
            if k == own and eng not in SELF_DEPS:
                continue
            if self.known[eng].get(k, 0) < v:
                self.known[eng][k] = v
                s = self.sem[k]
                self.code[eng].append(lambda e, s=s, v=v: e.wait_ge(s, v))

    def _deps(self, reads, writes):
        deps = []
        for b in reads:
            st = self.bufs.get(b)
            if st and st[0]:
                deps.append(st[0])
        for b in writes:
            st = self.bufs.get(b)
            if st:
                if st[0]:
                    deps.append(st[0])
                deps.extend(st[1])
        return deps

    def _mark(self, reads, writes, tick):
        for b in reads:
            self.bufs.setdefault(b, [None, []])[1].append(tick)
        for b in writes:
            self.bufs[b] = [tick, []]

    def op(self, eng, fn, reads=(), writes=()):
        if self.dry:
            return
        reads = _names(reads)
        writes = _names(writes)
        key = "e_" + eng
        self.getsem(key)
        self._need(eng, self._deps(reads, writes))
        self.val[key] += 1
        tick = (key, self.val[key])
        s = self.sem[key]
        self.code[eng].append(lambda e, fn=fn, s=s: fn(e).then_inc(s, 1))
        self._mark(reads, writes, tick)

    def dma(self, eng, fn, reads=(), writes=(), semkey=None, is_out=False):
        if self.dry:
            return
        reads = _names(reads)
        writes = _names(writes)
        self.getsem(semkey)
        self._need(eng, self._deps(reads, writes))
        self.val[semkey] += 16
        tick = (semkey, self.val[semkey])
        s = self.sem[semkey]
        self.code[eng].append(lambda e, fn=fn, s=s: fn(e).then_inc(s, 16))
        self._mark(reads, writes, tick)
        if is_out:
            self.out_ticks.append(tick)

    def finish(self, eng="sp"):
        if self.dry:
            return
        self._need(eng, self.out_ticks)


class WS:
    def __init__(self, P, ring, units=None):
        self.P = P
        self.ring = ring
        self.units = units
        self.seq = []
        self.ci = 0
        self.issued = 0

    def _issue(self, i):
        src, n = self.units[i]
        s = i % RING
        dst = self.ring[:, s, 0:n]
        self.P.dma("pool", lambda e, dst=dst, src=src: e.dma_start(out=dst, in_=src),
                   reads=(), writes=["ring.%d" % s], semkey="ring%d" % s)
        self.issued = i + 1

    def start(self):
        for i in range(min(RING, len(self.units))):
            self._issue(i)

    def acquire(self, src, n):
        i = self.ci
        if self.units is None:
            self.seq.append((src, n))
            return Buf(self.ring[:, i % RING, 0:n], ["ring.%d" % (i % RING)])
        assert self.units[i][1] == n
        assert i < self.issued
        return Buf(self.ring[:, i % RING, 0:n], ["ring.%d" % (i % RING)])

    def acquire_next(self, src, n):
        i = self.ci + 1
        if self.units is None:
            self.seq.append((src, n))
            return Buf(self.ring[:, i % RING, 0:n], ["ring.%d" % (i % RING)])
        assert self.units[i][1] == n and i < self.issued
        return Buf(self.ring[:, i % RING, 0:n], ["ring.%d" % (i % RING)])

    def release(self):
        i = self.ci
        self.ci += 1
        if self.units is not None and i + RING < len(self.units):
            self._issue(i + RING)


def _record(nc, P, ws, T, mode):
    xr, xb, big, S, Sb, Eall = T["xr"], T["xb"], T["big"], T["S"], T["Sb"], T["Eall"]
    mean_sb, rstd, tmpA, rb, rs = T["mean"], T["rstd"], T["tmpA"], T["rb"], T["rs"]
    kvc, sm, pT, prm, cst = T["kvc"], T["sm"], T["pT"], T["prm"], T["cst"]
    lnab, lbt, oml, identb, onesd, onesv, uprev, uh = (T["lnab"], T["lb"], T["oml"], T["identb"],
                                                       T["onesd"], T["onesv"], T["uprev"], T["uh"])
    psum = T["psum"]
    D_ = T["dram"]

    XOFF = 1 if mode == "F" else 0
    def XR(m):
        return Buf(xr[:, m, :], ["xr.%d" % m])

    def XB(m):
        return Buf(xb[:, m, :], ["xb.%d" % m])

    XB_ALL = ["xb.%d" % m for m in range(KC)]
    XR_ALL = ["xr.%d" % m for m in range(KC)]

    def cells(off, n):
        return ["big.%d" % c for c in range(off // 512, (off + n + 511) // 512)]

    def arena(off, shape, dt):
        n = int(np.prod(shape))
        nb = n * (2 if dt == F32 else 1)
        ap = big[:, off:off + nb]
        if dt == F32:
            ap = ap.bitcast(F32)
        if len(shape) == 2:
            ap = ap.rearrange("p (a b) -> p a b", b=shape[1])
        return Buf(ap, cells(off, nb))

    def sub(buf, idx, off_el_per, dt):
        raise NotImplementedError

    def bank(i):
        return Buf(psum[:, i * 512:(i + 1) * 512], ["ps%d" % i])

    def bank_bf(i):
        return Buf(psum[:, i * 512:(i + 1) * 512].bitcast(BF16), ["ps%d" % i])

    ident = Buf(cst[:, 0:128], ["cst"])
    mask = Buf(cst[0:64, 128:640], ["cst"])
    rmask = Buf(cst[:, 640:1152], ["cst"])
    IDB = Buf(identb[:], ["identb"])
    ONESD = Buf(onesd[:], ["onesd"])
    ONESV = Buf(onesv[:], ["onesv"])

    def pcol(c):
        return prm[:, c:c + 1]

    LNG0, LNB0, CW0, A00, A10, NW0, SEL0 = 0, 64, 128, 152, 160, 168, 169

    O_XIN = 0
    O_H = 0
    O_T = 0
    O_QT, O_KT, O_KHT, O_VT, O_YA, O_YB = 10240, 14336, 18432, 22528, 26624, 30720
    O_MRG = 10240

    def hbuf(j):
        return Buf(big[:, O_H + j * 512:O_H + (j + 1) * 512], ["big.%d" % (O_H // 512 + j)])

    def chunked(off, c):
        return Buf(big[:, off + c * 512:off + (c + 1) * 512], ["big.%d" % (off // 512 + c)])

    def chunked32(off, c):
        o = off + c * 1024
        return Buf(big[:, o:o + 1024].bitcast(F32), cells(o, 1024))

    def pe_multi(groups, reads, writes):
        def fn(e, groups=groups):
            ins = None
            for out_ap, pairs in groups:
                n = len(pairs)
                for i, (l, r) in enumerate(pairs):
                    ins = e.matmul(out_ap, l, r, start=(i == 0), stop=(i == n - 1))
            return ins
        P.op("pe", fn, reads, writes)

    def pe_transposes(items, reads, writes):
        def fn(e, items=items):
            ins = None
            for o, i_, idn in items:
                ins = e.transpose(o, i_, idn)
            return ins
        P.op("pe", fn, reads, writes)

    def act(out, in_, func, bias=0.0, scale=1.0, extra_reads=()):
        P.op("act", lambda e: e.activation(out.ap, in_.ap, func, bias=bias, scale=scale),
             [in_] + list(extra_reads), [out])

    def tt(out, a, b, op, eng="dve"):
        P.op(eng, lambda e: e.tensor_tensor(out.ap, a.ap, b.ap, op), [a, b], [out])

    def ts2(out, a, s1, s2, op0, op1, extra_reads=(), eng="dve"):
        P.op(eng, lambda e: e.tensor_scalar(out.ap, a.ap, s1, s2, op0, op1), [a] + list(extra_reads), [out])

    def ts1(out, a, s1, op, extra_reads=(), eng="dve"):
        P.op(eng, lambda e: e.tensor_single_scalar(out.ap, a.ap, s1, op), [a] + list(extra_reads), [out])

    def stt(out, a, s, b, op0, op1, extra_reads=(), eng="dve"):
        P.op(eng, lambda e: e.scalar_tensor_tensor(out.ap, a.ap, s, b.ap, op0, op1),
             [a, b] + list(extra_reads), [out])

    def cp(out, in_, eng="dve"):
        P.op(eng, lambda e: e.tensor_copy(out.ap, in_.ap), [in_], [out])

    def ln_prep(m):
        i = m % 2
        rbi = Buf(rb[:, i, :], ["rb.%d" % i])
        rsi = Buf(rs[:, i, :], ["rs.%d" % i])
        act(rbi, XR(m), AF.Copy)
        act(rsi, XR(m), AF.Square)

    def ln_mm(m):
        i = m % 2
        rbi = Buf(rb[:, i, :], ["rb.%d" % i])
        rsi = Buf(rs[:, i, :], ["rs.%d" % i])
        b6, b7 = bank(6), bank(7)

        def fn(e, m=m):
            e.matmul(b6.ap, onesd[:], rbi.ap, start=(m == 0), stop=(m == KC - 1), skip_group_check=True)
            return e.matmul(b7.ap, onesd[:], rsi.ap, start=(m == 0), stop=(m == KC - 1), skip_group_check=True)
        P.op("pe", fn, [rbi, rsi, ONESD], [b6, b7])

    def ln_step(m):
        if m >= 1:
            ln_mm(m - 1)

    def pipelined_groups(specs):
        ws_ = [(sl.ap.rearrange("p (k c) -> p k c", c=512), bk, cs) for sl, bk, cs in specs]
        slots = []
        for sl, _, _ in specs:
            if sl not in slots:
                slots.append(sl)
        for k in range(KC):
            def fn(e, k=k):
                ins = None
                for w, bk, cs in ws_:
                    ins = e.matmul(bk.ap, w[:, k, cs], xb[:, k, :], start=(k == 0), stop=(k == KC - 1),
                                   skip_group_check=True)
                return ins
            P.op("pe", fn, slots + ["xb.%d" % k], [bk for _, bk, _ in specs])

    def ln_apply(l, final=False, cast_next=False):
        MEAN = Buf(mean_sb[:], ["mean"])
        RSTD = Buf(rstd[:], ["rstd"])
        act(RSTD, bank(6), AF.Square)
        act(MEAN, bank(6), AF.Copy)
        stt(RSTD, bank(7), LN_EPS, RSTD, ALU.add, ALU.subtract)
        act(RSTD, RSTD, AF.Ln)
        act(RSTD, RSTD, AF.Exp, scale=-0.5)
        for m in range(KC):
            i = m % 2
            t = Buf(tmpA[:, i, :], ["tmpA.%d" % i])
            tt(t, XR(m), MEAN, ALU.subtract)
            stt(t, t, pcol(LNG0 + l * 16 + m), RSTD, ALU.mult, ALU.mult, extra_reads=["prm"])
            if not final:
                act(XB(m), t, AF.Identity, bias=pcol(LNB0 + l * 16 + m), scale=1.0, extra_reads=["prm"])
                act(XR(m), t, AF.Identity, bias=lnab[:, l * 16 + m:l * 16 + m + 1], scale=ALPHA,
                    extra_reads=["lnab"])
            else:
                act(XR(m), t, AF.Identity, bias=pcol(LNB0 + l * 16 + m), scale=1.0, extra_reads=["prm"])
                if cast_next:
                    load_x_cast(m)

    def ffn(f, l, pipelined=False, tail=None):
        wfi, wfo = D_["wfi%d" % f], D_["wfo%d" % f]

        def evac(j, pa, pu):
            sa = Buf(tmpA[:, j % 2, :], ["tmpA.%d" % (j % 2)])
            act(sa, pa, AF.Silu)
            tt(hbuf(j), sa, pu, ALU.mult)

        u0 = 0
        if pipelined:
            s0 = ws.acquire(wfi[0], 8192)
            s1 = ws.acquire_next(wfi[1], 8192)
            specs = []
            for uu, sl in ((0, s0), (1, s1)):
                for jj in range(2):
                    b0 = uu * 4 + jj * 2
                    specs.append((sl, bank(b0), slice(jj * 128, (jj + 1) * 128)))
                    specs.append((sl, bank(b0 + 1), slice(256 + jj * 128, 256 + (jj + 1) * 128)))
            pipelined_groups(specs)
            for j in range(4):
                evac(j, bank(2 * j), bank(2 * j + 1))
            ws.release()
            ws.release()
            u0 = 2
        for u in range(u0, 22):
            slot = ws.acquire(wfi[u], 8192)
            w = slot.ap.rearrange("p (k c) -> p k c", c=512)
            for jj in range(2):
                j = 2 * u + jj
                if j >= HC:
                    break
                pa = bank(0 if j % 2 == 0 else 2)
                pu = bank(1 if j % 2 == 0 else 3)
                groups = [
                    (pa.ap, [(w[:, k, jj * 128:(jj + 1) * 128], xb[:, k, :]) for k in range(KC)]),
                    (pu.ap, [(w[:, k, 256 + jj * 128:256 + (jj + 1) * 128], xb[:, k, :]) for k in range(KC)]),
                ]
                pe_multi(groups, [slot] + XB_ALL, [pa, pu])
                evac(j, pa, pu)
            ws.release()
        hall = ["big.%d" % (O_H // 512 + j) for j in range(HC)]
        for m in range(KC):
            slot = ws.acquire(wfo[m], DFF)
            w = slot.ap.rearrange("p (k c) -> p k c", c=128)
            po = bank(4 + m % 2)
            pe_multi([(po.ap, [(w[:, k, :], hbuf(k).ap) for k in range(HC)])], [slot] + hall, [po])
            ws.release()
            ln_step(m)
            stt(XR(m), po, 0.5, XR(m), ALU.mult, ALU.add)
            ln_prep(m)
        if tail is not None:
            tail()
        ln_mm(KC - 1)
        ln_apply(l)

    O_XIN2 = 16384

    def xin_bufs():
        return [Buf(big[:, O_XIN2 + b * 4096:O_XIN2 + (b + 1) * 4096].bitcast(F32), cells(O_XIN2 + b * 4096, 4096))
                for b in range(4)]

    def xin_fm():
        return [Buf(big[:, O_XIN2 + g * 4096:O_XIN2 + (g + 1) * 4096].bitcast(F32).rearrange("p (m j) -> p m j", j=NT),
                    cells(O_XIN2 + g * 4096, 4096)) for g in range(4)]

    def load_x_dma(t):
        x_d = D_["x"]
        t = t + XOFF
        xin = xin_fm()
        for g in range(4):
            src = x_d[t][:, g * 4:(g + 1) * 4, :]
            P.dma("sp", lambda e, o=xin[g].ap, s=src: e.dma_start(out=o, in_=s), [], [xin[g]], semkey="xin%d" % g)

    def xin_chunk(m):
        xin = xin_fm()
        return Buf(xin[m // 4].ap[:, m % 4, :], xin[m // 4].names)

    def load_x_cast(m):
        act(XB(m), xin_chunk(m), AF.Copy)

    def load_x_scale():
        for m in range(KC):
            ts1(XR(m), xin_chunk(m), ALPHA, ALU.mult)

    def load_x_tr():
        for m in range(KC):
            load_x_cast(m)
        load_x_scale()

    def load_x(t):
        load_x_dma(t)
        load_x_tr()

    def store_y(t):
        y_d = D_["y"]
        for g in range(4):
            P.dma("sp", lambda e, g=g: e.dma_start(out=y_d[t][:, g * 4:(g + 1) * 4, :], in_=xr[:, g * 4:(g + 1) * 4, :]),
                  ["xr.%d" % (g * 4 + mm) for mm in range(4)], [], semkey="yout%d" % g, is_out=True)

    TMP = [chunked32(O_T + 4096, i) for i in range(6)]
    EC = [chunked32(O_T, i) for i in range(4)]

    def QT(c):
        return chunked(O_QT, c)

    def KT(c):
        return chunked(O_KT, c)

    def KHT(c):
        return chunked(O_KHT, c)

    def VT(c):
        return chunked(O_VT, c)

    def zgroup(slot, jj, bk):
        w = slot.ap.rearrange("p (k c) -> p k c", c=512)
        pe_multi([(bk.ap, [(w[:, k, jj * 128:(jj + 1) * 128], xb[:, k, :]) for k in range(KC)])],
                 [slot] + XB_ALL, [bk])

    def f_head(c, slot, jj, full):
        bk = bank(c % 4)
        zgroup(slot, jj, bk)
        T1, T2, T3, T4, T5, T6 = TMP
        ecc = EC[c % 4] if full else T4
        act(T1, bk, AF.Sigmoid)
        ts2(T1, T1, oml[:, c:c + 1], lbt[:, c:c + 1], ALU.mult, ALU.add, extra_reads=["lb"])
        act(T2, T1, AF.Ln)
        P.op("dve", lambda e: e.tensor_tensor_scan(T3.ap, rmask.ap, T2.ap, 0.0, ALU.mult, ALU.add),
             [rmask, T2], [T3])
        act(ecc, T3, AF.Exp)
        act(T5, T3, AF.Exp, scale=-1.0)
        ts2(T1, T1, -1.0, 1.0, ALU.mult, ALU.add)
        tt(T2, T1, T5, ALU.mult)
        if full:
            act(KT(c), T2, AF.Copy)
        ec3 = ecc.ap.rearrange("p (c j) -> p c j", j=CH)
        ends = ec3[:, :, CH - 1:CH]
        cp(Buf(Eall[:, c, :], ["Eall"]), Buf(ec3[:, :, CH - 1], ecc.names))
        k3 = Buf(T2.ap.rearrange("p (c j) -> p c j", j=CH), T2.names)
        o3 = Buf(KHT(c).ap.rearrange("p (c j) -> p c j", j=CH), KHT(c).names)
        tt(o3, k3, Buf(ends.to_broadcast([128, NCH, CH]), ecc.names), ALU.mult)

    def f_unit(hh, slot, full):
        A = TMP[0:4]
        Bf = EC
        X, Y = TMP[4], TMP[5]
        for jj in range(4):
            c = hh * 4 + jj
            bk = bank(jj)
            zgroup(slot, jj, bk)
            act(A[jj], bk, AF.Sigmoid)
            ts2(A[jj], A[jj], oml[:, c:c + 1], lbt[:, c:c + 1], ALU.mult, ALU.add, extra_reads=["lb"])
        for jj in range(4):
            act(Bf[jj], A[jj], AF.Ln)
            ts2(A[jj], A[jj], -1.0, 1.0, ALU.mult, ALU.add)
        XS = [TMP[4], Buf(tmpA[:, 0, :], ["tmpA.0"])]
        YS = [TMP[5], Buf(tmpA[:, 1, :], ["tmpA.1"])]
        for jj in range(4):
            c = hh * 4 + jj
            X, Y = XS[jj % 2], YS[jj % 2]
            P.op("dve", lambda e, jj=jj, X=X: e.tensor_tensor_scan(X.ap, rmask.ap, Bf[jj].ap, 0.0, ALU.mult, ALU.add),
                 [rmask, Bf[jj]], [X])
            act(Bf[jj], X, AF.Exp)
            act(Y, X, AF.Exp, scale=-1.0)
            tt(A[jj], A[jj], Y, ALU.mult)
            if full:
                act(KT(c), A[jj], AF.Copy)
            ec3 = Bf[jj].ap.rearrange("p (c j) -> p c j", j=CH)
            ends = ec3[:, :, CH - 1:CH]
            cp(Buf(Eall[:, c, :], ["Eall"]), Buf(ec3[:, :, CH - 1], Bf[jj].names))
            k3 = Buf(A[jj].ap.rearrange("p (c j) -> p c j", j=CH), A[jj].names)
            o3 = Buf(KHT(c).ap.rearrange("p (c j) -> p c j", j=CH), KHT(c).names)
            tt(o3, k3, Buf(ends.to_broadcast([128, NCH, CH]), Bf[jj].names), ALU.mult)

    def q_head(c, slot, jj):
        bk = bank(4 + c % 4)
        zgroup(slot, jj, bk)
        T6 = TMP[5]
        act(T6, bk, AF.Silu)
        tt(QT(c), T6, EC[c % 4], ALU.mult)

    def v_head(c, slot, jj, k):
        bk = bank(c % 4)
        zgroup(slot, jj, bk)
        if k % 2 == 0:
            act(VT(c), bk, AF.Copy)
        else:
            cp(VT(c), bk)

    O_O = O_T

    def scan(full):
        KHT_ALL = cells(O_KHT, 4096)
        VT_ALL = cells(O_VT, 4096)
        QT_ALL = cells(O_QT, 4096)
        KT_ALL = cells(O_KT, 4096)
        O_ALL = cells(O_O, 8192)
        o3 = big[:, O_O:O_O + 8192].bitcast(F32).rearrange("p (c t) -> p c t", t=NT)
        SB_ = Buf(S[:], ["S"])
        def transposes(cc):
            i = cc % 2
            ptk, ptv = bank_bf(0), bank_bf(1)
            items = [(ptk.ap[0:CH, c * 128:(c + 1) * 128], big[:, O_KHT + c * 512 + cc * CH:O_KHT + c * 512 + (cc + 1) * CH],
                      identb[:]) for c in range(8)]
            items += [(ptv.ap[0:CH, c * 128:(c + 1) * 128], big[:, O_VT + c * 512 + cc * CH:O_VT + c * 512 + (cc + 1) * CH],
                       identb[:]) for c in range(8)]
            pe_transposes(items, KHT_ALL + VT_ALL + [IDB], [ptk, ptv])
            khc = Buf(kvc[0:CH, i, :], ["kvc.%d" % i])
            vc = Buf(kvc[0:CH, 2 + i, :], ["kvc.%d" % (2 + i)])
            act(khc, Buf(ptk.ap[0:CH, :], ptk.names), AF.Copy)
            cp(vc, Buf(ptv.ap[0:CH, :], ptv.names))

        transposes(0)
        for cc in range(NCH):
            sl = slice(cc * CH, (cc + 1) * CH)
            i = cc % 2
            khc = Buf(kvc[0:CH, i, :], ["kvc.%d" % i])
            vc = Buf(kvc[0:CH, 2 + i, :], ["kvc.%d" % (2 + i)])
            pU = Buf(psum[:, 4 * 512:6 * 512], ["ps4", "ps5"])
            groups = [(pU.ap[:, c * 128:(c + 1) * 128],
                       [(khc.ap[:, c * 128:(c + 1) * 128], vc.ap[:, c * 128:(c + 1) * 128])]) for c in range(8)]
            pe_multi(groups, [khc, vc], [pU])
            if full:
                pS = bank(2)
                groups = [(pS.ap[0:CH, c * CH:(c + 1) * CH],
                           [(big[:, O_KT + c * 512 + cc * CH:O_KT + c * 512 + (cc + 1) * CH],
                             big[:, O_QT + c * 512 + cc * CH:O_QT + c * 512 + (cc + 1) * CH])]) for c in range(8)]
                pe_multi(groups, KT_ALL + QT_ALL, [pS])
                smi = Buf(sm[0:CH, i, :], ["sm.%d" % i])
                tt(smi, Buf(pS.ap[0:CH, :], pS.names), mask, ALU.mult)
            if cc + 1 < NCH:
                transposes(cc + 1)
            if full:
                pO = bank(3)
                cur = cc % 2
                groups = []
                for c in range(8):
                    groups.append((pO.ap[:, c * CH:(c + 1) * CH],
                                   [(Sb[:, cur, c * 128:(c + 1) * 128],
                                     big[:, O_QT + c * 512 + cc * CH:O_QT + c * 512 + (cc + 1) * CH]),
                                    (vc.ap[:, c * 128:(c + 1) * 128], smi.ap[:, c * CH:(c + 1) * CH])]))
                pe_multi(groups, ["Sb.%d" % cur, vc, smi] + QT_ALL, [pO])
                act(Buf(o3[:, :, sl], O_ALL), Buf(pO.ap.rearrange("p (c t) -> p c t", t=CH), pO.names), AF.Copy)
            S3 = Buf(S[:].rearrange("p (c v) -> p c v", v=128), ["S"])
            eb = Buf(Eall[:, :, cc:cc + 1].to_broadcast([128, 8, 128]), ["Eall"])
            tt(S3, S3, eb, ALU.mult)
            tt(SB_, SB_, pU, ALU.add)
            if full:
                nxt = (cc + 1) % 2
                act(Buf(Sb[:, nxt, :], ["Sb.%d" % nxt]), SB_, AF.Copy)

    def mixer_a(halo):
        wmin = D_["wmin"]
        for hh in range(2):
            slot = ws.acquire(wmin[8 + hh], 8192)
            f_unit(hh, slot, full=False)
            ws.release()
        k = 0
        for hh in range(2):
            slot = ws.acquire(wmin[10 + hh], 8192)
            for jj in range(4):
                v_head(hh * 4 + jj, slot, jj, k)
                k += 1
            ws.release()
        scan(full=False)
        if halo:
            cs = chunked32(O_T + 4096, 0)
            c3 = Buf(cs.ap[:, 0:16].rearrange("p (c t) -> p c t", t=2), cs.names)
            for which in range(2):
                for hh in range(2):
                    slot = ws.acquire(wmin[2 + 2 * which + hh], 8192)
                    w = slot.ap.rearrange("p (k c) -> p k c", c=512)
                    bk = bank(hh)
                    groups = [(bk.ap[:, jj * 2:jj * 2 + 2],
                               [(w[:, k, jj * 128:(jj + 1) * 128], xb[:, k, NT - 2:NT]) for k in range(KC)])
                              for jj in range(4)]
                    pe_multi(groups, [slot] + XB_ALL, [bk])
                    ws.release()
                    src = Buf(bk.ap[:, 0:8].rearrange("p (c t) -> p c t", t=2), bk.names)
                    dst_c = Buf(c3.ap[:, hh * 4:(hh + 1) * 4, :], c3.names)
                    dst_u = Buf(uh[:, hh * 8:(hh + 1) * 8].rearrange("p (c t) -> p c t", t=2), ["uh"])
                    if which == 0:
                        cp(dst_c, src)
                    else:
                        tt(dst_u, src, dst_c, ALU.mult)

    def mixer_b(t):
        wmin, wbr, wmo = D_["wmin"], D_["wbr"], D_["wmo"]
        for hf in range(2):
            Cs = [chunked32(O_T, q) for q in range(4)]
            ubuf = [Buf(big[:, O_T + 4096 + q * 1032:O_T + 4096 + (q + 1) * 1032].bitcast(F32),
                        cells(O_T + 4096 + q * 1032, 1032)) for q in range(4)]
            slot = ws.acquire(wmin[2 + hf], 8192)
            if hf == 0:
                slot_h = ws.acquire_next(wmin[4 + hf], 8192)
                pipelined_groups([(slot, bank(q), slice(q * 128, (q + 1) * 128)) for q in range(4)] +
                                 [(slot_h, bank(4 + q), slice(q * 128, (q + 1) * 128)) for q in range(4)])
            for q in range(4):
                bk = bank(q)
                if hf != 0:
                    zgroup(slot, q, bk)
                act(Cs[q], bk, AF.Copy)
            ws.release()
            if hf != 0:
                slot_h = ws.acquire(wmin[4 + hf], 8192)
            for q in range(4):
                c = hf * 4 + q
                bk = bank(4 + q) if hf == 0 else bank(q)
                if hf != 0:
                    zgroup(slot_h, q, bk)
                ub = ubuf[q]
                tt(Buf(ub.ap[:, 2:2 + NT], ub.names), bk, Cs[q], ALU.mult)
                act(Buf(ub.ap[:, 0:2], ub.names), Buf(uprev[:, c * 2:c * 2 + 2], ["uprev"]), AF.Copy)
                ts1(Cs[q], Buf(ub.ap[:, 0:NT], ub.names), pcol(CW0 + 0 * 8 + c), ALU.mult, extra_reads=["prm"])
                stt(Cs[q], Buf(ub.ap[:, 1:1 + NT], ub.names), pcol(CW0 + 1 * 8 + c), Cs[q], ALU.mult, ALU.add,
                    extra_reads=["prm"])
                stt(Cs[q], Buf(ub.ap[:, 2:2 + NT], ub.names), pcol(CW0 + 2 * 8 + c), Cs[q], ALU.mult, ALU.add,
                    extra_reads=["prm"])
                act(Buf(uprev[:, c * 2:c * 2 + 2], ["uprev"]), Buf(ub.ap[:, NT:NT + 2], ub.names), AF.Copy)
            ws.release()
            slot = ws.acquire(wmin[0 + hf], 8192)
            for q in range(4):
                c = hf * 4 + q
                bk = bank(q)
                zgroup(slot, q, bk)
                tt(chunked(O_YA, c), bk, Cs[q], ALU.mult)
            ws.release()
        for hh in range(2):
            slot = ws.acquire(wmin[8 + hh], 8192)
            f_unit(hh, slot, full=True)
            ws.release()
            slot = ws.acquire(wmin[6 + hh], 8192)
            for jj in range(4):
                q_head(hh * 4 + jj, slot, jj)
            ws.release()
        k = 0
        for hh in range(2):
            slot = ws.acquire(wmin[10 + hh], 8192)
            for jj in range(4):
                v_head(hh * 4 + jj, slot, jj, k)
                k += 1
            ws.release()
        scan(full=True)
        ple_prep(t)
        O_OSQ, O_RSQ = O_QT, O_QT + 4096
        for c in range(8):
            act(chunked(O_OSQ, c), chunked32(O_O, c), AF.Square)
        for c in range(8):
            bm = bank(6 + c % 2)
            pe_multi([(bm.ap, [(onesv[:], chunked(O_OSQ, c).ap)])], [chunked(O_OSQ, c), ONESV], [bm])
            ts1(chunked32(O_RSQ, c), bm, RMS_EPS, ALU.add)
        for c in range(8):
            act(chunked32(O_RSQ, c), chunked32(O_RSQ, c), AF.Ln)
        for c in range(8):
            rsq = chunked32(O_RSQ, c)
            oc = chunked32(O_O, c)
            act(rsq, rsq, AF.Exp, scale=-0.5)
            stt(oc, oc, pcol(NW0), rsq, ALU.mult, ALU.mult, extra_reads=["prm"])
        for hh in range(2):
            slot = ws.acquire(wmin[12 + hh], 8192)
            for jj in range(4):
                c = hh * 4 + jj
                bk = bank(jj)
                zgroup(slot, jj, bk)
                sg = chunked32(O_O + 8192 + (c % 2) * 1024, 0)
                act(sg, bk, AF.Silu)
                tt(chunked(O_YB, c), chunked32(O_O, c), sg, ALU.mult)
            ws.release()
        YA_ALL = cells(O_YA, 4096)
        YB_ALL = cells(O_YB, 4096)
        for mg in range(4):
            sgt = [chunked32(O_T, q) for q in range(4)]
            mgt = [chunked32(O_T + 4096, q) for q in range(4)]
            for br in range(2):
                slot = ws.acquire(wmin[14 + 4 * br + mg], 8192)
                for q in range(4):
                    bk = bank(q)
                    zgroup(slot, q, bk)
                    act(sgt[q], bk, AF.Sigmoid)
                ws.release()
                slot = ws.acquire(wbr[mg * 2 + br], 4096)
                w = slot.ap.rearrange("p (k c) -> p k c", c=512)
                yoff = O_YA if br == 0 else O_YB
                for q in range(4):
                    m = mg * 4 + q
                    bk = bank(4 + q % 2)
                    pe_multi([(bk.ap, [(w[:, k, q * 128:(q + 1) * 128], big[:, yoff + k * 512:yoff + (k + 1) * 512])
                                       for k in range(8)])], [slot] + (YA_ALL if br == 0 else YB_ALL), [bk])
                    if br == 0:
                        tt(mgt[q], sgt[q], bk, ALU.mult)
                    else:
                        tt(sgt[q], sgt[q], bk, ALU.mult)
                        tt(chunked(O_MRG, m), sgt[q], mgt[q], ALU.add)
                ws.release()
        MRG_ALL = cells(O_MRG, 8192)
        for mq in range(4):
            slot = ws.acquire(wmo[mq], 8192)
            w = slot.ap.rearrange("p (k c) -> p k c", c=512)
            for q in range(4):
                m = mq * 4 + q
                bk = bank(4 + m % 2)
                pe_multi([(bk.ap, [(w[:, k, q * 128:(q + 1) * 128], big[:, O_MRG + k * 512:O_MRG + (k + 1) * 512])
                                   for k in range(KC)])], [slot] + MRG_ALL, [bk])
                ln_step(m)
                tt(XR(m), XR(m), bk, ALU.add)
                ln_prep(m)
            ws.release()
        ln_mm(KC - 1)
        ln_apply(1)

    O_PP = 16384

    def ple_prep(t):
        p_d = D_["p"]
        pin = Buf(kvc[:, 0:2, :].bitcast(F32), ["kvc.0", "kvc.1"])
        P.dma("sp", lambda e: e.dma_start(out=pin.ap, in_=p_d[t]), [], [pin], semkey="pin")
        for kc in range(2):
            act(Buf(pT[:, kc, :], ["pT"]), Buf(pin.ap[:, kc, :], pin.names), AF.Copy)

    def ple_pp():
        wpp = D_["wpp"]
        pslot = ws.acquire(wpp[0], 4096)
        wp = pslot.ap.rearrange("p (k c) -> p k c", c=2048)
        for m in range(KC):
            bb = bank(m % 4)
            pe_multi([(bb.ap, [(wp[:, k, m * 128:(m + 1) * 128], pT[:, k, :]) for k in range(2)])],
                     [pslot, "pT"], [bb])
            if m % 2 == 0:
                act(chunked32(O_PP, m), bb, AF.Copy)
            else:
                cp(chunked32(O_PP, m), bb)
        ws.release()

    def ple(t):
        wpg = D_["wpg"]
        for mq in range(4):
            slot = ws.acquire(wpg[mq], 8192)
            w = slot.ap.rearrange("p (k c) -> p k c", c=512)
            for q in range(4):
                m = mq * 4 + q
                ba = bank(m % 4)
                if mq == 0:
                    if q == 0:
                        pipelined_groups([(slot, bank(qq), slice(qq * 128, (qq + 1) * 128)) for qq in range(4)])
                else:
                    pe_multi([(ba.ap, [(w[:, k, q * 128:(q + 1) * 128], xb[:, k, :]) for k in range(KC)])],
                             [slot] + XB_ALL, [ba])
                ln_step(m)
                tg = Buf(tmpA[:, m % 2, :], ["tmpA.%d" % (m % 2)])
                act(tg, ba, AF.Sigmoid)
                tt(tg, tg, chunked32(O_PP, m), ALU.mult)
                tt(XR(m), XR(m), tg, ALU.add)
                ln_prep(m)
            ws.release()
        ln_mm(KC - 1)
        if t + 1 < NTILES:
            load_x_dma(t + 1)
        ln_apply(3, final=True, cast_next=(t + 1 < NTILES))

    P.dma("sp", lambda e: e.dma_start(out=prm[:], in_=D_["prm"]), [], ["prm"], semkey="prm")
    P.dma("sp", lambda e: e.dma_start(out=cst[:], in_=D_["cst"]), [], ["cst"], semkey="cst")
    P.op("dve", lambda e: e.tensor_single_scalar(lnab[:], prm[:, LNB0:LNB0 + 64], ALPHA, ALU.mult), ["prm"], ["lnab"])
    P.op("dve", lambda e: e.tensor_tensor(lbt[:], prm[:, A00:A00 + 8], prm[:, A10:A10 + 8], ALU.subtract), ["prm"], ["lb"])
    P.op("act", lambda e: e.activation(lbt[:], lbt[:], AF.Sigmoid), ["lb"], ["lb"])
    P.op("dve", lambda e: e.tensor_scalar(oml[:], lbt[:], -1.0, 1.0, ALU.mult, ALU.add), ["lb"], ["lb"])
    P.op("act", lambda e: e.activation(identb[:], cst[:, 0:128], AF.Copy), ["cst"], ["identb"])
    P.op("dve", lambda e: e.memset(onesd[:], 1.0 / D), [], ["onesd"])
    P.op("dve", lambda e: e.memset(onesv[:], 1.0 / 128.0), [], ["onesv"])
    P.op("dve", lambda e: e.memset(S[:], 0.0), [], ["S"])
    P.op("dve", lambda e: e.memset(uh[:], 0.0), [], ["uh"])
    if ws.units is not None:
        ws.start()

    if mode == "F":
        load_x(-1)
        ffn(1, 0, pipelined=True)
        mixer_a(True)
        SBF = Buf(S[:], ["S"])
        ts1(SBF, SBF, pcol(SEL0), ALU.mult, extra_reads=["prm"])
        ts1(Buf(uprev[:], ["uprev"]), Buf(uh[:], ["uh"]), pcol(SEL0), ALU.mult, extra_reads=["prm"])
        P.op("act", lambda e: e.activation(Sb[:, 0, :], S[:], AF.Copy), ["S"], ["Sb.0"])
        load_x_dma(0)
        for t in range(NTILES):
            if t == 0:
                load_x_tr()
            else:
                load_x_scale()
            ffn(1, 0, pipelined=True)
            mixer_b(t)
            ffn(2, 2, pipelined=True, tail=ple_pp)
            ple(t)
            store_y(t)
        P.finish("sp")
        return

    if mode in ("A", "AB"):
        for t in range(NTILES):
            load_x(t)
            ffn(1, 0)
            P.dma("sp", lambda e, t=t: e.dma_start(out=D_["xs"][t], in_=xr[:].rearrange("p a b -> p (a b)")),
                  XR_ALL, ["xs.%d" % t], semkey="spill", is_out=(mode == "A"))
            mixer_a(t == NTILES - 1)

    SB_ = Buf(S[:], ["S"])
    UP = Buf(uprev[:], ["uprev"])
    if mode == "A":
        ccin = D_["ccin"]
        P.dma("sp", lambda e: e.dma_start(out=ccin[:, 0:1024], in_=S[:]), ["S"], ["ccin"], semkey="ccs", is_out=True)
        P.dma("sp", lambda e: e.dma_start(out=ccin[:, 1024:CCW], in_=uh[:]), ["uh"], ["ccin"], semkey="ccs", is_out=True)
    elif mode == "B":
        gin = D_["gin"]
        P.dma("sp", lambda e: e.dma_start(out=S[:], in_=gin[:, 0:1024]), [], ["S"], semkey="gin")
        P.dma("sp", lambda e: e.dma_start(out=uprev[:], in_=gin[:, 1024:CCW]), [], ["uprev"], semkey="gin")
    else:
        ccin, ccout = D_["ccin"], D_["ccout"]
        P.dma("sp", lambda e: e.dma_start(out=ccin[:, 0:1024], in_=S[:]), ["S"], ["ccin"], semkey="ccs")
        P.dma("sp", lambda e: e.dma_start(out=ccin[:, 1024:CCW], in_=uh[:]), ["uh"], ["ccin"], semkey="ccs")
        P.dma("pool", lambda e: e.collective_compute("AllGather", ALU.bypass, replica_groups=[list(range(NCORES))],
                                                     ins=[ccin], outs=[ccout]),
              ["ccin"], ["ccout"], semkey="cc")
        G = arena(0, [NCORES, CCW], F32)
        P.dma("sp", lambda e: e.dma_start(out=G.ap, in_=ccout.rearrange("(r p) f -> p r f", p=128)),
              ["ccout"], [G], semkey="ccg")
        for r in range(NCORES):
            gs = Buf(G.ap[:, r, 0:1024], G.names)
            gu = Buf(G.ap[:, r, 1024:CCW], G.names)
            if r == 0:
                ts1(SB_, gs, pcol(SEL0 + r), ALU.mult, extra_reads=["prm"])
                ts1(UP, gu, pcol(SEL0 + r), ALU.mult, extra_reads=["prm"])
            else:
                stt(SB_, gs, pcol(SEL0 + r), SB_, ALU.mult, ALU.add, extra_reads=["prm"])
                stt(UP, gu, pcol(SEL0 + r), UP, ALU.mult, ALU.add, extra_reads=["prm"])

    if mode in ("B", "AB"):
        P.op("act", lambda e: e.activation(Sb[:, 0, :], S[:], AF.Copy), ["S"], ["Sb.0"])
        for t in range(NTILES):
            P.dma("sp", lambda e, t=t: e.dma_start(out=xr[:].rearrange("p a b -> p (a b)"), in_=D_["xs"][t]),
                  ["xs.%d" % t], XR_ALL, semkey="reload")
            for m in range(KC):
                act(XB(m), XR(m), AF.Copy, scale=1.0 / ALPHA)
            mixer_b(t)
            ffn(2, 2)
            ple(t)
            store_y(t)
    P.finish("sp")


def _ple_fix_note():
    pass


def build(mode):
    nc = bass.Bass("TRN2", target_bir_lowering=False)
    dram = {}

    def din(name, shape):
        dram[name] = nc.dram_tensor(name, shape, F32, kind="ExternalInput").ap()

    if mode in ("A", "AB"):
        din("x", [TOK, D])
    if mode == "F":
        din("x", [NTILES + 1, 128, KC, NT])
    din("prm", [128, 192])
    din("cst", [128, 1152])
    if mode in ("A", "AB", "F"):
        din("wfi1", [22, 128, 8192])
        din("wfo1", [16, 128, DFF])
    din("wmin", [22, 128, 8192])
    if mode in ("B", "AB", "F"):
        din("p", [NTILES, 128, 2, NT])
        din("wfi2", [22, 128, 8192])
        din("wfo2", [16, 128, DFF])
        din("wbr", [8, 128, 4096])
        din("wmo", [4, 128, 8192])
        din("wpg", [4, 128, 8192])
        din("wpp", [1, 128, 4096])
        dram["y"] = nc.dram_tensor("y", [NTILES, 128, KC, NT], F32, kind="ExternalOutput").ap()
    if mode == "A":
        dram["xs"] = nc.dram_tensor("xs", [NTILES, 128, 8192], F32, kind="ExternalOutput").ap()
        dram["ccin"] = nc.dram_tensor("ccin", [128, CCW], F32, kind="ExternalOutput").ap()
    elif mode == "B":
        din("xs", [NTILES, 128, 8192])
        din("gin", [128, CCW])
    elif mode == "AB":
        dram["xs"] = nc.dram_tensor("xs", [NTILES, 128, 8192], F32).ap()
        dram["ccin"] = nc.dram_tensor("ccin", [128, CCW], F32).ap()
        dram["ccout"] = nc.dram_tensor("ccout", [NCORES * 128, CCW], F32).ap()

    with ExitStack() as st:
        def sb(name, shape, dt):
            return st.enter_context(nc.sbuf_tensor("s_" + name, shape, dt))

        T = {"dram": dram}
        T["xr"] = sb("xr", [128, KC, NT], F32)
        T["xb"] = sb("xb", [128, KC, NT], BF16)
        T["big"] = sb("big", [128, 34816], BF16)
        ring = sb("ring", [128, RING, SLOT], BF16)
        T["S"] = sb("S", [128, 1024], F32)
        T["Sb"] = sb("Sb", [128, 2, 1024], BF16)
        T["Eall"] = sb("Eall", [128, 8, NCH], F32)
        T["mean"] = sb("mean", [128, NT], F32)
        T["rstd"] = sb("rstd", [128, NT], F32)
        T["tmpA"] = sb("tmpA", [128, 2, NT], F32)
        T["rb"] = sb("rb", [128, 2, NT], BF16)
        T["rs"] = sb("rs", [128, 2, NT], BF16)
        T["kvc"] = sb("kvc", [128, 4, 1024], BF16)
        T["sm"] = sb("sm", [128, 2, NT], BF16)
        T["pT"] = sb("pT", [128, 2, NT], BF16)
        T["prm"] = sb("prm", [128, 192], F32)
        T["cst"] = sb("cst", [128, 1152], F32)
        T["lnab"] = sb("lnab", [128, 64], F32)
        T["lb"] = sb("lb", [128, 8], F32)
        T["oml"] = sb("oml", [128, 8], F32)
        T["identb"] = sb("identb", [128, 128], BF16)
        T["onesd"] = sb("onesd", [128, 128], BF16)
        T["onesv"] = sb("onesv", [128, 128], BF16)
        T["uprev"] = sb("uprev", [128, 16], F32)
        T["uh"] = sb("uh", [128, 16], F32)
        T["psum"] = st.enter_context(nc.psum_tensor("psum", [128, 4096], F32))

        Pd = Prog(nc, st, dry=True)
        wsd = WS(Pd, ring, None)
        _record(nc, Pd, wsd, T, mode)
        units = wsd.seq
        P = Prog(nc, st, dry=False)
        ws = WS(P, ring, units)
        _record(nc, P, ws, T, mode)
        assert ws.ci == len(units), (ws.ci, len(units))

        with nc.Block() as block:
            @block.tensor
            def _(e):
                for f in P.code["pe"]:
                    f(e)

            @block.scalar
            def _(e):
                for f in P.code["act"]:
                    f(e)

            @block.vector
            def _(e):
                for f in P.code["dve"]:
                    f(e)

            @block.gpsimd
            def _(e):
                for f in P.code["pool"]:
                    f(e)

            @block.sync
            def _(e):
                for f in P.code["sp"]:
                    f(e)
    return nc


def _k_units(W, ncol):
    K, C = W.shape
    kc = K // 128
    u = C // ncol
    a = W.reshape(kc, 128, u, ncol).transpose(2, 1, 0, 3)
    return np.ascontiguousarray(a).reshape(u, 128, kc * ncol)


def _prep_shared(inp):
    out = {}
    for f, (wi, wo) in ((1, ("ffn1_w_in", "ffn1_w_out")), (2, ("ffn2_w_in", "ffn2_w_out"))):
        W = inp[wi][0]
        Wp = np.zeros((D, 2, 22 * 256), np.float32)
        Wp[:, 0, :DFF] = W[:, :DFF]
        Wp[:, 1, :DFF] = W[:, DFF:]
        a = Wp.reshape(KC, 128, 2, 22, 256).transpose(3, 1, 0, 2, 4)
        out["wfi%d" % f] = np.ascontiguousarray(a).reshape(22, 128, 8192)
        out["wfo%d" % f] = _k_units(inp[wo][0], 128)
    out["wmin"] = _k_units(inp["mix_w_in"][0], 512)
    wa = _k_units(inp["branch_w_conv"][0], 512)
    wb = _k_units(inp["branch_w_hgrn"][0], 512)
    out["wbr"] = np.ascontiguousarray(np.stack([wa, wb], axis=1)).reshape(8, 128, 4096)
    out["wmo"] = _k_units(inp["mix_w_out"][0], 512)
    out["wpg"] = _k_units(inp["ple_w_gate"][0], 512)
    out["wpp"] = _k_units(inp["ple_w_proj"][0], 2048)
    cst = np.zeros((128, 1152), np.float32)
    cst[:, 0:128] = np.eye(128, dtype=np.float32)
    s = np.arange(CH)[:, None]
    tq = np.arange(CH)[None, :]
    m = (s <= tq).astype(np.float32)
    cst[0:CH, 128:640] = np.tile(m, (1, 8))
    rm = np.ones(NT, np.float32)
    rm[::CH] = 0.0
    cst[:, 640:1152] = rm[None, :]
    out["cst"] = cst
    return out


def _prep_prm(inp, core):
    prm = np.zeros((128, 192), np.float32)
    g = inp["ln_g"][0].reshape(4, KC, 128)
    b = inp["ln_b"][0].reshape(4, KC, 128)
    prm[:, 0:64] = g.transpose(2, 0, 1).reshape(128, 64)
    prm[:, 64:128] = b.transpose(2, 0, 1).reshape(128, 64)
    cw = inp["conv_w"][0].reshape(3, 8, 128)
    prm[:, 128:152] = cw.transpose(2, 0, 1).reshape(128, 24)
    lbp = inp["hg_lower_bound"].reshape(2, 8, 128)
    prm[:, 152:160] = lbp[0].T
    prm[:, 160:168] = lbp[1].T
    prm[:, 168] = inp["hg_norm_w"][0]
    if core % 2 == 1:
        prm[:, 169] = 1.0
    return prm


_NC_CACHE = {}


def _get_nc(mode):
    if mode not in _NC_CACHE:
        _NC_CACHE[mode] = build(mode)
    return _NC_CACHE[mode]


F_KEYS = ("cst", "wfi1", "wfo1", "wmin", "wfi2", "wfo2", "wbr", "wmo", "wpg", "wpp")


def _to_fm(a, nchunk):
    T = a.shape[0] // NT
    return np.ascontiguousarray(a.reshape(T, NT, nchunk, 128).transpose(0, 3, 2, 1))


def kernel(**inputs):
    inp = {k: np.asarray(v) for k, v in inputs.items()}
    shared = _prep_shared(inp)
    xflat = np.ascontiguousarray(inp["x"]).reshape(NCORES * TOK, D)
    ps = np.ascontiguousarray(inp["p"][0]).reshape(NCORES, TOK, 256)
    in_maps = []
    for c in range(NCORES):
        m = {k: shared[k] for k in F_KEYS}
        lo = c * TOK
        if c % 2 == 1:
            xe = xflat[lo - NT:lo + TOK]
        else:
            xe = np.concatenate([xflat[lo:lo + NT], xflat[lo:lo + TOK]], axis=0)
        m["x"] = _to_fm(xe, KC)
        m["p"] = _to_fm(ps[c], 2)
        m["prm"] = _prep_prm(inp, c)
        in_maps.append(m)
    res = run_bass_kernel_spmd(_get_nc("F"), in_maps, core_ids=list(range(NCORES))).results
    outs = []
    for r in res:
        yf = np.asarray(r["y"])
        outs.append(yf.transpose(0, 3, 2, 1).reshape(TOK, D))
    y = np.stack(outs, axis=0)
    return np.ascontiguousarray(y.reshape(4, 4096, D)).astype(np.float32, copy=False)
```
